# Optimizing a Trainium2 kernel written in Bass

```python
import math
import jax, jax.numpy as jnp
from jax import lax
import numpy as np

D_MODEL = 1024
BATCH = 4
SEQ = 8192
DEPTH = 4

N_MIXERS = 2
N_ATTN_LAYERS = (DEPTH + 1) // 2
N_GMLP_LAYERS = DEPTH // 2
D_FF = 2816
DIFF_HEADS = 8
DIFF_HEAD_DIM = D_MODEL // DIFF_HEADS // 2
DIFF_V_DIM = 2 * DIFF_HEAD_DIM
DIFF_QK_WIDTH = DIFF_HEADS * 2 * DIFF_HEAD_DIM
DIFF_V_WIDTH = DIFF_HEADS * DIFF_V_DIM
Q_BLOCK = 128
GMLP_HALF = 2 * D_MODEL
GMLP_GROUPS = 8
GMLP_CHUNK = 128
RMS_EPS = 1e-6
LN_EPS = 1e-5

kernel_name = "hybrid_diffattn_chunked_sgu_macaron"


def _rmsnorm(x, g):
    xf = x.astype(jnp.float32)
    y = xf * lax.rsqrt(jnp.mean(xf * xf, axis=-1, keepdims=True) + RMS_EPS)
    return (y * g.astype(jnp.float32)).astype(x.dtype)


def _layernorm(x, g, b):
    xf = x.astype(jnp.float32)
    mu = jnp.mean(xf, axis=-1, keepdims=True)
    xc = xf - mu
    var = jnp.mean(xc * xc, axis=-1, keepdims=True)
    y = xc * lax.rsqrt(var + LN_EPS) * g.astype(jnp.float32) + b.astype(jnp.float32)
    return y.astype(x.dtype)


def _swiglu_ffn(x, w_gate_up, w_down):
    g, u = jnp.split(x @ w_gate_up, 2, axis=-1)
    return (jax.nn.silu(g) * u) @ w_down


def _lambda_init(layer_idx):
    return 0.8 - 0.6 * math.exp(-0.3 * layer_idx)


def _diff_attention(h, w_in, w_out, q_norm, k_norm, lq1, lk1, lq2, lk2, subln, lam_init):
    B, S, _ = h.shape
    H, d, dv = DIFF_HEADS, DIFF_HEAD_DIM, DIFF_V_DIM
    q, k, v = jnp.split(h @ w_in, [DIFF_QK_WIDTH, 2 * DIFF_QK_WIDTH], axis=-1)
    q = _rmsnorm(q.reshape(B, S, H, 2, d), q_norm) * (d ** -0.5)
    k = _rmsnorm(k.reshape(B, S, H, 2, d), k_norm)
    v = v.reshape(B, S, H, dv)
    lam = (jnp.exp(jnp.sum(lq1.astype(jnp.float32) * lk1.astype(jnp.float32)))
           - jnp.exp(jnp.sum(lq2.astype(jnp.float32) * lk2.astype(jnp.float32)))
           + lam_init)
    outs = []
    for i in range(S // Q_BLOCK):
        L = (i + 1) * Q_BLOCK
        q_blk = q[:, i * Q_BLOCK:L]
        s = jnp.einsum('bqhcd,bkhcd->bhcqk', q_blk, k[:, :L]).astype(jnp.float32)
        qpos = i * Q_BLOCK + jnp.arange(Q_BLOCK)
        kpos = jnp.arange(L)
        mask = kpos[None, :] <= qpos[:, None]
        p = jax.nn.softmax(jnp.where(mask, s, -jnp.inf), axis=-1)
        a = (p[:, :, 0] - lam * p[:, :, 1]).astype(v.dtype)
        outs.append(jnp.einsum('bhqk,bkhe->bqhe', a, v[:, :L]))
    o = jnp.concatenate(outs, axis=1)
    o = _rmsnorm(o, subln) * (1.0 - lam_init)
    return o.reshape(B, S, DIFF_V_WIDTH) @ w_out


def _chunked_sgu(h, w_in, b_in, ln_g, ln_b, w_s, b_s, w_out, b_out):
    B, S, _ = h.shape
    z = jax.nn.gelu(h @ w_in + b_in, approximate=False)
    u, v = jnp.split(z, 2, axis=-1)
    v = _layernorm(v, ln_g, ln_b)
    nc = S // GMLP_CHUNK
    gc = GMLP_HALF // GMLP_GROUPS
    v = v.reshape(B, nc, GMLP_CHUNK, GMLP_GROUPS, gc)
    causal = jnp.tril(jnp.ones((GMLP_CHUNK, GMLP_CHUNK), dtype=bool))
    w = jnp.where(causal, w_s, 0.0)
    s = jnp.einsum('gts,bnsgc->bntgc', w, v) + jnp.transpose(b_s)[None, None, :, :, None]
    gated = u * s.reshape(B, S, GMLP_HALF)
    return gated @ w_out + b_out


def setup_inputs(seed: int = 0) -> dict:
    key = jax.random.key(seed)
    ks = jax.random.split(key, 32)
    f32 = jnp.float32
    nrm = lambda k, shape, scale: jax.random.normal(k, shape, f32) * scale
    gain = lambda k, shape: 1.0 + 0.02 * jax.random.normal(k, shape, f32)
    NA, NG = N_ATTN_LAYERS, N_GMLP_LAYERS
    return {
        "x": jax.random.normal(ks[0], (BATCH, SEQ, D_MODEL), f32),
        "ffn1_norm": gain(ks[1], (DEPTH, D_MODEL)),
        "ffn1_w_gate_up": nrm(ks[2], (DEPTH, D_MODEL, 2 * D_FF), D_MODEL ** -0.5),
        "ffn1_w_down": nrm(ks[3], (DEPTH, D_FF, D_MODEL), D_FF ** -0.5),
        "mix_norm": gain(ks[4], (DEPTH, D_MODEL)),
        "ffn2_norm": gain(ks[5], (DEPTH, D_MODEL)),
        "ffn2_w_gate_up": nrm(ks[6], (DEPTH, D_MODEL, 2 * D_FF), D_MODEL ** -0.5),
        "ffn2_w_down": nrm(ks[7], (DEPTH, D_FF, D_MODEL), D_FF ** -0.5),
        "attn_w_in": nrm(ks[8], (NA, D_MODEL, 2 * DIFF_QK_WIDTH + DIFF_V_WIDTH), D_MODEL ** -0.5),
        "attn_w_out": nrm(ks[9], (NA, DIFF_V_WIDTH, D_MODEL), DIFF_V_WIDTH ** -0.5),
        "attn_q_norm": gain(ks[10], (NA, DIFF_HEAD_DIM)),
        "attn_k_norm": gain(ks[11], (NA, DIFF_HEAD_DIM)),
        "attn_lambda_q1": nrm(ks[12], (NA, DIFF_HEAD_DIM), 0.1),
        "attn_lambda_k1": nrm(ks[13], (NA, DIFF_HEAD_DIM), 0.1),
        "attn_lambda_q2": nrm(ks[14], (NA, DIFF_HEAD_DIM), 0.1),
        "attn_lambda_k2": nrm(ks[15], (NA, DIFF_HEAD_DIM), 0.1),
        "attn_subln": gain(ks[16], (NA, DIFF_V_DIM)),
        "gmlp_w_in": nrm(ks[17], (NG, D_MODEL, 2 * GMLP_HALF), D_MODEL ** -0.5),
        "gmlp_b_in": nrm(ks[18], (NG, 2 * GMLP_HALF), 0.02),
        "gmlp_ln_g": gain(ks[19], (NG, GMLP_HALF)),
        "gmlp_ln_b": nrm(ks[20], (NG, GMLP_HALF), 0.02),
        "gmlp_w_s": nrm(ks[21], (NG, GMLP_GROUPS, GMLP_CHUNK, GMLP_CHUNK), 0.5 * GMLP_CHUNK ** -0.5),
        "gmlp_b_s": gain(ks[22], (NG, GMLP_GROUPS, GMLP_CHUNK)),
        "gmlp_w_out": nrm(ks[23], (NG, GMLP_HALF, D_MODEL), GMLP_HALF ** -0.5),
        "gmlp_b_out": nrm(ks[24], (NG, D_MODEL), 0.02),
    }


def reference(x, ffn1_norm, ffn1_w_gate_up, ffn1_w_down, mix_norm, ffn2_norm, ffn2_w_gate_up,
              ffn2_w_down, attn_w_in, attn_w_out, attn_q_norm, attn_k_norm, attn_lambda_q1,
              attn_lambda_k1, attn_lambda_q2, attn_lambda_k2, attn_subln, gmlp_w_in, gmlp_b_in,
              gmlp_ln_g, gmlp_ln_b, gmlp_w_s, gmlp_b_s, gmlp_w_out, gmlp_b_out):
    for i in range(DEPTH):
        x = x + 0.5 * _swiglu_ffn(_rmsnorm(x, ffn1_norm[i]), ffn1_w_gate_up[i], ffn1_w_down[i])
        h = _rmsnorm(x, mix_norm[i])
        j = i // N_MIXERS
        if i % N_MIXERS == 0:
            x = x + _diff_attention(h, attn_w_in[j], attn_w_out[j], attn_q_norm[j], attn_k_norm[j],
                                    attn_lambda_q1[j], attn_lambda_k1[j], attn_lambda_q2[j],
                                    attn_lambda_k2[j], attn_subln[j], _lambda_init(i))
        else:
            x = x + _chunked_sgu(h, gmlp_w_in[j], gmlp_b_in[j], gmlp_ln_g[j], gmlp_ln_b[j],
                                 gmlp_w_s[j], gmlp_b_s[j], gmlp_w_out[j], gmlp_b_out[j])
        x = x + 0.5 * _swiglu_ffn(_rmsnorm(x, ffn2_norm[i]), ffn2_w_gate_up[i], ffn2_w_down[i])
    return x
```

```python
import contextlib
import numpy as np
import concourse.bass as bass
import concourse.mybir as mybir
from concourse.bass_utils import run_bass_kernel_spmd

F32 = mybir.dt.float32
BF16 = mybir.dt.bfloat16
AF = mybir.ActivationFunctionType
ALU = mybir.AluOpType
AX = mybir.AxisListType

ENGS = ["pe", "act", "dve", "pool", "sp"]
R_DMA = 8


class Rec:
    def __init__(self, nc):
        self.nc = nc
        self.ops = []
        self.last_w = {}
        self.readers = {}
        self.arena_rd = {}
        self.pending = {e: set() for e in ENGS}
        self.last_op = {e: None for e in ENGS}
        self.recent_dma = {e: [] for e in ENGS}

    def op(self, eng, fn, reads=(), writes=(), dma=False, arena_reads=(), arena_writes=()):
        i = len(self.ops)
        deps = set()
        for r in reads:
            if r in self.last_w:
                deps.add(self.last_w[r])
        for w in writes:
            if w in self.last_w:
                deps.add(self.last_w[w])
            for rd in self.readers.get(w, {}).values():
                deps.add(rd)
        for a in arena_writes:
            for rd in self.arena_rd.get(a, {}).values():
                deps.add(rd)
        for a in arena_reads:
            self.arena_rd.setdefault(a, {})[eng if not dma else ("d", i)] = i
        for r in reads:
            self.readers.setdefault(r, {})[eng if not dma else ("d", i)] = i
        for w in writes:
            self.last_w[w] = i
            self.readers[w] = {}
        if self.pending[eng]:
            deps |= self.pending[eng]
            self.pending[eng] = set()
        deps.discard(i)
        self.ops.append(dict(eng=eng, fn=fn, deps=deps, dma=dma, marked=False))
        self.last_op[eng] = i
        if dma:
            self.recent_dma[eng].append(i)
            self.recent_dma[eng] = self.recent_dma[eng][-R_DMA:]
        return i

    def fence(self):
        deps = set()
        for e in ENGS:
            if self.last_op[e] is not None:
                deps.add(self.last_op[e])
            deps |= set(self.recent_dma[e])
        for e in ENGS:
            self.pending[e] |= deps

    def emit(self, final_wait_all=True):
        nc = self.nc
        ops = self.ops
        for o in ops:
            real = []
            for d in sorted(o["deps"]):
                p = ops[d]
                if (not p["dma"]) and (not o["dma"]) and p["eng"] == o["eng"] == "pe":
                    continue
                real.append(d)
                p["marked"] = True
            o["real"] = real
        cnt = {e: 0 for e in ENGS}
        dcnt = {e: 0 for e in ENGS}
        for o in ops:
            e = o["eng"]
            if o["dma"]:
                n = dcnt[e]
                dcnt[e] += 1
                o["sem"] = ("d", e, n % R_DMA)
                o["val"] = 16 * (n // R_DMA + 1)
                o["throttle"] = (("d", e, n % R_DMA), 16 * (n // R_DMA)) if n >= R_DMA else None
            elif o["marked"]:
                cnt[e] += 1
                o["sem"] = ("c", e)
                o["val"] = cnt[e]
        per_eng = {e: [o for o in ops if o["eng"] == e] for e in ENGS}
        semkeys = [("c", e) for e in ENGS if cnt[e] > 0]
        for e in ENGS:
            for k in range(min(dcnt[e], R_DMA)):
                semkeys.append(("d", e, k))
        self.stats = dict(cnt=dict(cnt), dcnt=dict(dcnt), nops={e: len(per_eng[e]) for e in ENGS})
        with contextlib.ExitStack() as st:
            semh = {}
            for k in semkeys:
                semh[k] = st.enter_context(nc.semaphore("s_" + "_".join(str(x) for x in k)))
            block = st.enter_context(nc.Block())

            def mk(engname):
                def body(eng):
                    waited = {}

                    def do_wait(s, v):
                        if v <= 0 or waited.get(s, 0) >= v:
                            return
                        waited[s] = v
                        eng.wait_ge(semh[s], v)

                    for o in per_eng[engname]:
                        if o["dma"] and o["throttle"]:
                            do_wait(*o["throttle"])
                        for d in o["real"]:
                            do_wait(ops[d]["sem"], ops[d]["val"])
                        ins = o["fn"](eng)
                        if o["dma"]:
                            ins.then_inc(semh[o["sem"]], 16)
                        elif o["marked"]:
                            ins.then_inc(semh[o["sem"]], 1)
                    if engname == "sp" and final_wait_all:
                        for e in ENGS:
                            n = dcnt[e]
                            for k in range(min(n, R_DMA)):
                                tot = len(range(k, n, R_DMA))
                                do_wait(("d", e, k), 16 * tot)
                return body

            block.tensor(mk("pe"))
            block.scalar(mk("act"))
            block.vector(mk("dve"))
            block.gpsimd(mk("pool"))
            block.sync(mk("sp"))


class SB:
    def __init__(self, nc, limit=None):
        self.nc = nc
        self.base = (nc.SBUF_PARTITION_SIZE_BYTES - nc.sbuf_bytes_remaining + 63) // 64 * 64
        self.limit = nc.SBUF_PARTITION_SIZE_BYTES - self.base
        self.off = 0
        self.n = 0

    def alloc(self, shape, dtype, name=None):
        esz = 2 if dtype == BF16 else 4
        free = int(np.prod(shape[1:])) * esz
        off = (self.off + 63) // 64 * 64
        assert off + free <= self.limit, f"SBUF overflow {off}+{free}>{self.limit}"
        self.off = off + free
        self.n += 1
        nm = f"{name or 't'}_{self.n}"
        return self.nc.alloc_sbuf_tensor_at(nm, list(shape), dtype, offset=self.base + off)

    def mark(self):
        return self.off

    def reset(self, m):
        self.off = m


D = 1024
KC = D // 128
FF = 2816
FH = FF // 2
JH = FH // 128
ARENA_BYTES = 66 * 1024 + 2048
RMS_EPS = 1e-6


class Ctx:
    def __init__(self, nc, ntok):
        self.nc = nc
        self.ntok = ntok
        self.rec = Rec(nc)
        self.sb = SB(nc)
        self.ident = self.sb.alloc([128, 128], BF16, "ident")
        self.tril = self.sb.alloc([128, 128], F32, "tril")
        self.triu = self.sb.alloc([128, 128], BF16, "triu")
        self.ones = self.sb.alloc([128, 128], BF16, "ones")
        self.blk64 = self.sb.alloc([128, 128], BF16, "blk64")
        self.arena = [self.sb.alloc([128, ARENA_BYTES // 2], BF16, f"arena{a}") for a in range(2)]
        self.work0 = self.sb.mark()
        self.bank = [nc.alloc_psum_tensor(f"bank{i}", [128, 512], F32) for i in range(8)]

    def arena_view(self, a, off_elems, shape):
        n = int(np.prod(shape))
        ap = self.arena[a][:, off_elems:off_elems + n]
        if len(shape) == 2:
            ap = ap.rearrange("p (a b) -> p a b", a=shape[0])
        return ap


def load_consts(cx, ident_dram, tril_dram=None, cb=None):
    rec = cx.rec
    rec.op("sp", lambda e: e.dma_start(out=cx.ident[:], in_=ident_dram), writes=[("ident",)], dma=True)
    if cb is not None:
        rec.op("sp", lambda e: e.dma_start(out=cx.triu[:], in_=cb[0]), writes=[("triu",)], dma=True)
        rec.op("sp", lambda e: e.dma_start(out=cx.ones[:], in_=cb[1]), writes=[("ones",)], dma=True)
        rec.op("sp", lambda e: e.dma_start(out=cx.blk64[:], in_=cb[2]), writes=[("blk64",)], dma=True)
    if tril_dram is not None:
        rec.op("sp", lambda e: e.dma_start(out=cx.tril[:], in_=tril_dram), writes=[("tril",)], dma=True)


class FFNHalf:
    def __init__(self, cx, a, w_gu, w_down, norm, half, nsrc, res, dst, tag):
        self.cx, self.a, self.half, self.tag = cx, a, half, tag
        self.w_gu, self.w_down, self.norm = w_gu, w_down, norm
        (self.nsrc_n, self.nsrc), (self.res_n, self.res), (self.dst_n, self.dst) = nsrc, res, dst
        self.Wg = cx.arena_view(a, 0, [KC, FH])
        self.Wu = cx.arena_view(a, KC * FH, [KC, FH])
        self.Wd = cx.arena_view(a, 2 * KC * FH, [JH, D])
        self.gT = cx.arena[a][:, 2 * KC * FH + JH * D: 2 * KC * FH + JH * D + 2 * KC].bitcast(F32)

    def load(self):
        cx, rec, a, t = self.cx, self.cx.rec, self.a, self.tag
        c0 = self.half * FH
        gu = self.w_gu.rearrange("(k p) f -> p k f", p=128)
        for kp in range(KC // 2):
            ks = slice(2 * kp, 2 * kp + 2)
            rec.op("pool", lambda e, ks=ks: e.dma_start(out=self.Wg[:, ks, :], in_=gu[:, ks, c0:c0 + FH]),
                   writes=[("Wg", t, kp)], dma=True, arena_writes=[a])
            rec.op("pool", lambda e, ks=ks: e.dma_start(out=self.Wu[:, ks, :], in_=gu[:, ks, FF + c0:FF + c0 + FH]),
                   writes=[("Wu", t, kp)], dma=True, arena_writes=[a])
        wd = self.w_down.rearrange("(j p) d -> p j d", p=128)
        j0 = self.half * JH
        for q in range(3):
            js = slice(4 * q, min(4 * q + 4, JH))
            rec.op("pool", lambda e, js=js: e.dma_start(out=self.Wd[:, js, :], in_=wd[:, j0 + js.start:j0 + js.stop, :]),
                   writes=[("Wd", t, q)], dma=True, arena_writes=[a])
        rec.op("sp", lambda e: e.dma_start(out=self.gT, in_=self.norm.rearrange("(k p) -> p k", p=128),
                                           allow_slow_non_contiguous=True),
               writes=[("gT", t)], dma=True, arena_writes=[a])

    def compute(self):
        cx, rec, a, t = self.cx, self.cx.rec, self.a, self.tag
        nc = cx.nc
        sb = cx.sb
        sb.reset(cx.work0)
        xin = [sb.alloc([128, D], F32, "xin") for _ in range(3)]
        junk = sb.alloc([128, D], BF16, "junk")
        ss = [sb.alloc([128, 1], F32, "ss") for _ in range(3)]
        rstd = [sb.alloc([128, 1], F32, "rstd") for _ in range(3)]
        hn = [sb.alloc([128, D], BF16, "hn") for _ in range(4)]
        hT = [sb.alloc([128, KC, 512], BF16, "hT") for _ in range(2)]
        sg = [sb.alloc([128, 512], F32, "sg") for _ in range(2)]
        aT = sb.alloc([128, JH, 512], BF16, "aT")
        xr = [sb.alloc([128, D], F32, "xr") for _ in range(3)]
        psT = [cx.bank[0][:].bitcast(BF16).rearrange("p (k n) -> p k n", k=KC),
               cx.bank[1][:].bitcast(BF16).rearrange("p (k n) -> p k n", k=KC)]
        psG = [cx.bank[2], cx.bank[3]]
        psU = [cx.bank[4], cx.bank[5]]
        psD = [cx.bank[6], cx.bank[7]]
        ntile = cx.ntok // 512
        self.dbg = dict(hT=hT, aT=aT, hn=hn, rstd=rstd, ss=ss)

        def prep_nonpe(ti):
            for b in range(4):
                g = ti * 4 + b
                s = g % 3
                r0 = g * 128
                rec.op("sp", lambda e, s=s, r0=r0: e.dma_start(out=xin[s][:], in_=self.nsrc[r0:r0 + 128, :]),
                       reads=[("X", self.nsrc_n, g)], writes=[("xin", s)], dma=True)
                rec.op("act", lambda e, s=s: e.activation(out=junk[:], in_=xin[s][:], func=AF.Square,
                                                          accum_out=ss[s][:, 0:1]),
                       reads=[("xin", s)], writes=[("ss", s)])
                rec.op("act", lambda e, s=s: e.activation(out=ss[s][:, 0:1], in_=ss[s][:, 0:1], func=AF.Sqrt,
                                                          bias=RMS_EPS, scale=1.0 / D),
                       reads=[("ss", s)], writes=[("ss", s)])
                rec.op("dve", lambda e, s=s: e.reciprocal(out=rstd[s][:], in_=ss[s][:]),
                       reads=[("ss", s)], writes=[("rstd", s)])
                h = g % 4
                rec.op("dve", lambda e, s=s, h=h: e.tensor_scalar(out=hn[h][:], in0=xin[s][:], scalar1=rstd[s][:, 0:1],
                                                                  scalar2=None, op0=ALU.mult),
                       reads=[("xin", s), ("rstd", s)], writes=[("hn", h)])

        def prep_pe(ti):
            tb = ti % 2
            for b in range(4):
                g = ti * 4 + b
                h = g % 4
                p = g % 2
                for kc in range(KC):
                    rec.op("pe", lambda e, h=h, p=p, kc=kc: e.transpose(out=psT[p][:, kc, :],
                                                                       in_=hn[h][:, kc * 128:(kc + 1) * 128],
                                                                       identity=cx.ident[:]),
                           reads=[("hn", h), ("ident",)], writes=[("psT", p)])
                rec.op("dve", lambda e, p=p, tb=tb, b=b: e.tensor_tensor(
                    out=hT[tb][:, :, b * 128:(b + 1) * 128], in0=psT[p],
                    in1=self.gT.unsqueeze(2).broadcast_to([128, KC, 128]), op=ALU.mult),
                    reads=[("psT", p), ("gT", t)], writes=[("hT", tb, b)], arena_reads=[a])

        def gate_up(ti):
            tb = ti % 2
            for j in range(JH):
                s = j % 2
                for (W, ps, nm) in ((self.Wg, psG, "Wg"), (self.Wu, psU, "Wu")):
                    for kc in range(KC):
                        rec.op("pe", lambda e, W=W, ps=ps, kc=kc, j=j, s=s: e.matmul(
                            ps[s][:], lhsT=W[:, kc, j * 128:(j + 1) * 128], rhs=hT[tb][:, kc, :],
                            start=(kc == 0), stop=(kc == KC - 1)),
                            reads=[(nm, t, kc // 2)] + [("hT", tb, b) for b in range(4)],
                            writes=[("ps" + nm, s)], arena_reads=[a])
                rec.op("act", lambda e, s=s: e.activation(out=sg[s][:], in_=psG[s][:], func=AF.Silu),
                       reads=[("psWg", s)], writes=[("sg", s)])
                rec.op("dve", lambda e, s=s, j=j: e.tensor_tensor(out=aT[:, j, :], in0=sg[s][:], in1=psU[s][:],
                                                                  op=ALU.mult),
                       reads=[("sg", s), ("psWu", s)], writes=[("aT", j)])

        def down(ti):
            for b in range(4):
                g = ti * 4 + b
                s = g % 3
                r0 = g * 128
                rec.op("sp", lambda e, s=s, r0=r0: e.dma_start(out=xr[s][:], in_=self.res[r0:r0 + 128, :]),
                       reads=[("X", self.res_n, g)], writes=[("xr", s), ("xo", s, 0), ("xo", s, 1)], dma=True)
                for dh in range(2):
                    for j in range(JH):
                        rec.op("pe", lambda e, dh=dh, j=j, b=b: e.matmul(
                            psD[dh][:], lhsT=aT[:, j, b * 128:(b + 1) * 128], rhs=self.Wd[:, j, dh * 512:(dh + 1) * 512],
                            start=(j == 0), stop=(j == JH - 1)),
                            reads=[("aT", j), ("Wd", t, j // 4)], writes=[("psD", dh)], arena_reads=[a])
                    rec.op("dve", lambda e, dh=dh, s=s: e.scalar_tensor_tensor(
                        out=xr[s][:, dh * 512:(dh + 1) * 512], in0=psD[dh][:], scalar=0.5,
                        in1=xr[s][:, dh * 512:(dh + 1) * 512], op0=ALU.mult, op1=ALU.add),
                        reads=[("psD", dh), ("xr", s)], writes=[("xo", s, dh)])
                rec.op("sp", lambda e, s=s, r0=r0: e.dma_start(out=self.dst[r0:r0 + 128, :], in_=xr[s][:]),
                       reads=[("xo", s, 0), ("xo", s, 1)], writes=[("X", self.dst_n, g)], dma=True)

        prep_nonpe(0)
        prep_pe(0)
        for ti in range(ntile):
            if ti + 1 < ntile:
                prep_nonpe(ti + 1)
            gate_up(ti)
            if ti + 1 < ntile:
                prep_pe(ti + 1)
            down(ti)


def rms_prep(cx, rec, src_rows, xin, ss, rstd, hn, key, dres=()):
    rec.op("sp", lambda e: e.dma_start(out=xin[:], in_=src_rows), reads=list(dres), writes=[("xin", key)], dma=True)
    rec.op("act", lambda e: e.activation(out=cx.junk[:, 0:D], in_=xin[:], func=AF.Square, accum_out=ss[:, 0:1]),
           reads=[("xin", key)], writes=[("ss", key)])
    rec.op("act", lambda e: e.activation(out=ss[:, 0:1], in_=ss[:, 0:1], func=AF.Sqrt, bias=RMS_EPS, scale=1.0 / D),
           reads=[("ss", key)], writes=[("ss", key)])
    rec.op("dve", lambda e: e.reciprocal(out=rstd[:], in_=ss[:]), reads=[("ss", key)], writes=[("rstd", key)])


GH = 2048
LN_EPS = 1e-5


class GMLP:
    def __init__(self, cx, w_in, b_in, ln_g, ln_b, w_s, b_s, w_out, b_out, norm, src, dst, tag):
        self.cx, self.tag = cx, tag
        self.p = dict(w_in=w_in, b_in=b_in, ln_g=ln_g, ln_b=ln_b, w_s=w_s, b_s=b_s, w_out=w_out, b_out=b_out, norm=norm)
        (self.src_n, self.src), (self.dst_n, self.dst) = src, dst
        A0, A1 = cx.arena[0], cx.arena[1]
        self.Win = A0[:, 0:KC * 4096].rearrange("p (k f) -> p k f", k=KC)
        self.bout = A0[:, KC * 4096:KC * 4096 + 2 * D].bitcast(F32)
        o = 0
        self.Wout = A1[:, o:o + 16 * D].rearrange("p (k f) -> p k f", k=16); o += 16 * D
        self.wsT = A1[:, o:o + 8 * 128].rearrange("p (g t) -> p g t", g=8); o += 8 * 128
        self.bin = A1[:, o:o + 2 * 4096].bitcast(F32); o += 2 * 4096
        self.lng = A1[:, o:o + 2 * GH].bitcast(F32); o += 2 * GH
        self.lnb = A1[:, o:o + 2 * GH].bitcast(F32); o += 2 * GH
        assert o <= ARENA_BYTES // 2

    def load(self):
        cx, rec, t, p = self.cx, self.cx.rec, self.tag, self.p
        win = p["w_in"].rearrange("(k p) f -> p k f", p=128)
        for kc in range(KC):
            rec.op("pool", lambda e, kc=kc: e.dma_start(out=self.Win[:, kc, :], in_=win[:, kc, :]),
                   writes=[("gWin", t, kc)], dma=True, arena_writes=[0])
        wout = p["w_out"].rearrange("(k p) f -> p k f", p=128)
        for q in range(4):
            rec.op("pool", lambda e, q=q: e.dma_start(out=self.Wout[:, 4 * q:4 * q + 4, :], in_=wout[:, 4 * q:4 * q + 4, :]),
                   writes=[("gWout", t, q)], dma=True, arena_writes=[1])
        for (nm, dst_ap, src_ap, ar) in (("bin", self.bin, p["b_in"], 1), ("lng", self.lng, p["ln_g"], 1),
                                         ("lnb", self.lnb, p["ln_b"], 1), ("bout", self.bout, p["b_out"], 0)):
            rec.op("sp", lambda e, dst_ap=dst_ap, src_ap=src_ap: e.dma_start(out=dst_ap, in_=src_ap.partition_broadcast(128)),
                   writes=[("g" + nm, t)], dma=True, arena_writes=[ar])

    def compute(self):
        cx, rec, t, p = self.cx, self.cx.rec, self.tag, self.p
        sb = cx.sb
        sb.reset(cx.work0)
        cx.junk = sb.alloc([128, GH], BF16, "junk")
        gT = sb.alloc([128, KC], F32, "gT")
        bsT = sb.alloc([128, 8], F32, "bsT")
        wtmp = [sb.alloc([128, 128], F32, "wtmp") for _ in range(2)]
        wtmpb = [sb.alloc([128, 128], BF16, "wtmpb") for _ in range(2)]
        xin = [sb.alloc([128, D], F32, "xin") for _ in range(3)]
        ss = [sb.alloc([128, 1], F32, "ss") for _ in range(3)]
        rstd = [sb.alloc([128, 1], F32, "rstd") for _ in range(3)]
        hn = [sb.alloc([128, D], BF16, "hn") for _ in range(2)]
        hT = [sb.alloc([128, KC, 128], BF16, "hT") for _ in range(2)]
        zb = [sb.alloc([128, 512], F32, "zb") for _ in range(2)]
        u = sb.alloc([128, GH], F32, "u")
        v = sb.alloc([128, GH], F32, "v")
        vsum = sb.alloc([128, 4], F32, "vsum")
        st = sb.alloc([128, 4], F32, "st")
        vn = sb.alloc([128, GH], F32, "vn")
        vnb = sb.alloc([128, GH], BF16, "vnb")
        gated = sb.alloc([128, GH], BF16, "gated")
        gatedT = sb.alloc([128, 16, 128], BF16, "gatedT")
        bank = cx.bank
        psT = bank[0][:].bitcast(BF16).rearrange("p (k n) -> p k n", k=KC)
        psZ = [bank[1], bank[2]]
        psS = [bank[3], bank[4]]
        psGT = bank[5][:].bitcast(BF16).rearrange("p (k n) -> p k n", k=8)
        psO = [bank[6], bank[7]]
        nblk = cx.ntok // 128

        rec.op("sp", lambda e: e.dma_start(out=gT[:], in_=p["norm"].rearrange("(k p) -> p k", p=128),
                                           allow_slow_non_contiguous=True), writes=[("gT", t)], dma=True)
        rec.op("sp", lambda e: e.dma_start(out=bsT[:], in_=p["b_s"].rearrange("g t -> t g"),
                                           allow_slow_non_contiguous=True), writes=[("bsT", t)], dma=True)
        for g in range(8):
            k = g % 2
            rec.op("sp", lambda e, g=g, k=k: e.dma_start(out=wtmp[k][:], in_=p["w_s"][g]), writes=[("wtmp", k)], dma=True)
            rec.op("dve", lambda e, k=k: e.tensor_tensor(out=wtmpb[k][:], in0=wtmp[k][:], in1=cx.tril[:], op=ALU.mult),
                   reads=[("wtmp", k), ("tril",)], writes=[("wtmpb", k)])
            rec.op("pe", lambda e, k=k: e.transpose(out=psT[:, 0, :], in_=wtmpb[k][:], identity=cx.ident[:]),
                   reads=[("wtmpb", k), ("ident",)], writes=[("psT", "h")])
            rec.op("act", lambda e, g=g, k=k: e.copy(out=self.wsT[:, g, :], in_=psT[:, 0, :]),
                   reads=[("psT", "h")], writes=[("wsT", t, g)], arena_writes=[1])

        def A(b):
            s3, s2 = b % 3, b % 2
            rms_prep(cx, rec, self.src[b * 128:(b + 1) * 128, :], xin[s3], ss[s3], rstd[s3], None, s3, [("X", self.src_n, b)])
            rec.op("dve", lambda e: e.tensor_scalar(out=hn[s2][:], in0=xin[s3][:], scalar1=rstd[s3][:, 0:1],
                                                    scalar2=None, op0=ALU.mult),
                   reads=[("xin", s3), ("rstd", s3)], writes=[("hn", s2)])
            for kc in range(KC):
                rec.op("pe", lambda e, kc=kc: e.transpose(out=psT[:, kc, :], in_=hn[s2][:, kc * 128:(kc + 1) * 128],
                                                          identity=cx.ident[:]),
                       reads=[("hn", s2), ("ident",)], writes=[("psT", "h")])
            rec.op("dve", lambda e: e.tensor_tensor(out=hT[s2][:], in0=psT,
                                                    in1=gT[:].unsqueeze(2).broadcast_to([128, KC, 128]), op=ALU.mult),
                   reads=[("psT", "h"), ("gT", t)], writes=[("hT", s2)])
            rec.op("pool", lambda e: e.tensor_tensor(out=xin[s3][:], in0=xin[s3][:], in1=self.bout, op=ALU.add),
                   reads=[("xin", s3), ("hn", s2), ("gbout", t)], writes=[("xin", s3)], arena_reads=[0])

        def Bz(b, slabs):
            s2 = b % 2
            for c in slabs:
                k = c % 2
                for kc in range(KC):
                    rec.op("pe", lambda e, c=c, kc=kc, k=k: e.matmul(psZ[k][:], lhsT=hT[s2][:, kc, :],
                                                                    rhs=self.Win[:, kc, c * 512:(c + 1) * 512],
                                                                    start=(kc == 0), stop=(kc == KC - 1)),
                           reads=[("hT", s2), ("gWin", t, kc)], writes=[("psZ", k)], arena_reads=[0])
                rec.op("dve", lambda e, c=c, k=k: e.tensor_tensor(out=zb[k][:], in0=psZ[k][:],
                                                                  in1=self.bin[:, c * 512:(c + 1) * 512], op=ALU.add),
                       reads=[("psZ", k), ("gbin", t)], writes=[("zb", k)], arena_reads=[1])
                if c < 4:
                    rec.op("act", lambda e, c=c, k=k: e.activation(out=u[:, c * 512:(c + 1) * 512], in_=zb[k][:], func=AF.Gelu),
                           reads=[("zb", k)], writes=[("u", c)])
                else:
                    rec.op("act", lambda e, c=c, k=k: e.activation(out=v[:, (c - 4) * 512:(c - 3) * 512], in_=zb[k][:],
                                                                   func=AF.Gelu, accum_out=vsum[:, c - 4:c - 3]),
                           reads=[("zb", k)], writes=[("v", c - 4), ("vsum", c - 4)])

        def C(b):
            allv = [("v", i) for i in range(4)]
            rec.op("dve", lambda e: e.reduce_sum(out=st[:, 0:1], in_=vsum[:], axis=AX.X),
                   reads=[("vsum", i) for i in range(4)], writes=[("st", 0)])
            rec.op("dve", lambda e: e.tensor_scalar(out=st[:, 1:2], in0=st[:, 0:1], scalar1=-1.0 / GH, scalar2=None, op0=ALU.mult),
                   reads=[("st", 0)], writes=[("st", 1)])
            rec.op("act", lambda e: e.activation(out=cx.junk[:], in_=v[:], func=AF.Square, bias=st[:, 1:2],
                                                 accum_out=st[:, 2:3]),
                   reads=allv + [("st", 1)], writes=[("st", 2)])
            rec.op("act", lambda e: e.activation(out=st[:, 2:3], in_=st[:, 2:3], func=AF.Sqrt, bias=LN_EPS, scale=1.0 / GH),
                   reads=[("st", 2)], writes=[("st", 2)])
            rec.op("dve", lambda e: e.reciprocal(out=st[:, 3:4], in_=st[:, 2:3]), reads=[("st", 2)], writes=[("st", 3)])
            rec.op("dve", lambda e: e.tensor_scalar(out=vn[:], in0=v[:], scalar1=st[:, 1:2], scalar2=st[:, 3:4],
                                                    op0=ALU.add, op1=ALU.mult),
                   reads=allv + [("st", 1), ("st", 3)], writes=[("vn",)])
            rec.op("pool", lambda e: e.tensor_tensor(out=vn[:], in0=vn[:], in1=self.lng, op=ALU.mult),
                   reads=[("vn",), ("glng", t)], writes=[("vn",)], arena_reads=[1])
            rec.op("pool", lambda e: e.tensor_tensor(out=vnb[:], in0=vn[:], in1=self.lnb, op=ALU.add),
                   reads=[("vn",), ("glnb", t)], writes=[("vnb",)], arena_reads=[1])

        def Dg(b):
            for pr in range(4):
                k = pr % 2
                for g in (2 * pr, 2 * pr + 1):
                    rec.op("pe", lambda e, g=g, k=k: e.matmul(psS[k][:, (g % 2) * 256:(g % 2 + 1) * 256], lhsT=self.wsT[:, g, :],
                                                              rhs=vnb[:, g * 256:(g + 1) * 256], start=True, stop=True),
                           reads=[("vnb",), ("wsT", t, g)], writes=[("psS", k)], arena_reads=[1])
                for g in (2 * pr, 2 * pr + 1):
                    rec.op("dve", lambda e, g=g, k=k: e.scalar_tensor_tensor(
                        out=gated[:, g * 256:(g + 1) * 256], in0=psS[k][:, (g % 2) * 256:(g % 2 + 1) * 256],
                        scalar=bsT[:, g:g + 1], in1=u[:, g * 256:(g + 1) * 256], op0=ALU.add, op1=ALU.mult),
                        reads=[("psS", k), ("bsT", t), ("u", g // 2)], writes=[("gated", g // 4, g % 4)])

        def E(b):
            for hh in range(2):
                for i in range(8):
                    fc = hh * 8 + i
                    rec.op("pe", lambda e, fc=fc, i=i: e.transpose(out=psGT[:, i, :], in_=gated[:, fc * 128:(fc + 1) * 128],
                                                                  identity=cx.ident[:]),
                           reads=[("gated", fc // 8, j) for j in range(4)] + [("ident",)], writes=[("psGT",)])
                rec.op("act", lambda e, hh=hh: e.copy(out=gatedT[:, hh * 8:(hh + 1) * 8, :], in_=psGT),
                       reads=[("psGT",)], writes=[("gatedT", hh)])

        def Fo(b):
            s3 = b % 3
            for dh in range(2):
                for fc in range(16):
                    rec.op("pe", lambda e, dh=dh, fc=fc: e.matmul(psO[dh][:], lhsT=gatedT[:, fc, :],
                                                                  rhs=self.Wout[:, fc, dh * 512:(dh + 1) * 512],
                                                                  start=(fc == 0), stop=(fc == 15)),
                           reads=[("gatedT", fc // 8), ("gWout", t, fc // 4)], writes=[("psO", dh)], arena_reads=[1])
                rec.op("dve", lambda e, dh=dh: e.tensor_tensor(out=xin[s3][:, dh * 512:(dh + 1) * 512], in0=psO[dh][:],
                                                               in1=xin[s3][:, dh * 512:(dh + 1) * 512], op=ALU.add),
                       reads=[("psO", dh), ("xin", s3)], writes=[("xin", s3)])
            rec.op("sp", lambda e: e.dma_start(out=self.dst[b * 128:(b + 1) * 128, :], in_=xin[s3][:]),
                   reads=[("xin", s3)], writes=[("X", self.dst_n, b)], dma=True)

        A(0)
        Bz(0, [4, 5, 6, 7, 0, 1, 2, 3])
        for b in range(nblk):
            nxt = b + 1 < nblk
            if nxt:
                A(b + 1)
            C(b)
            if nxt:
                Bz(b + 1, [4, 5, 6, 7])
            Dg(b)
            if nxt:
                Bz(b + 1, [0, 1, 2, 3])
            E(b)
            Fo(b)


NH = 8
HD = 64


class Attn:
    def __init__(self, cx, w_in, w_out, q_norm, k_norm, lq1, lk1, lq2, lk2, subln, lam_init, norm, src, dst, scr, tag):
        self.cx, self.tag = cx, tag
        self.p = dict(w_in=w_in, w_out=w_out, q_norm=q_norm, k_norm=k_norm, lq1=lq1, lk1=lk1, lq2=lq2, lk2=lk2,
                      subln=subln, norm=norm)
        self.lam_init = float(lam_init)
        (self.src_n, self.src), (self.dst_n, self.dst) = src, dst
        self.QT, self.KT, self.V, self.OT = scr
        A0, A1 = cx.arena[0], cx.arena[1]
        self.Win = A0[:, 0:KC * 3072].rearrange("p (k f) -> p k f", k=KC)
        self.Wout = A1[:, 0:NH * D].rearrange("p (k f) -> p k f", k=NH)

    def load(self):
        cx, rec, t, p = self.cx, self.cx.rec, self.tag, self.p
        win = p["w_in"].rearrange("(k p) f -> p k f", p=128)
        for kc in range(KC):
            rec.op("pool", lambda e, kc=kc: e.dma_start(out=self.Win[:, kc, :], in_=win[:, kc, :]),
                   writes=[("aWin", t, kc)], dma=True, arena_writes=[0])
        wout = p["w_out"].rearrange("(k p) f -> p k f", p=128)
        for q in range(2):
            rec.op("pool", lambda e, q=q: e.dma_start(out=self.Wout[:, 4 * q:4 * q + 4, :], in_=wout[:, 4 * q:4 * q + 4, :]),
                   writes=[("aWout", t, q)], dma=True, arena_writes=[1])

    def proj(self):
        cx, rec, t, p = self.cx, self.cx.rec, self.tag, self.p
        sb = cx.sb
        sb.reset(cx.work0)
        cx.junk = sb.alloc([128, D], BF16, "junk")
        gT = sb.alloc([128, KC], F32, "gT")
        gqk = sb.alloc([128, 2], F32, "gqk")
        xin = [sb.alloc([128, D], F32, "xin") for _ in range(3)]
        ss = [sb.alloc([128, 1], F32, "ss") for _ in range(3)]
        rstd = [sb.alloc([128, 1], F32, "rstd") for _ in range(3)]
        hn = [sb.alloc([128, D], BF16, "hn") for _ in range(2)]
        hT = [sb.alloc([128, KC, 512], BF16, "hT") for _ in range(2)]
        sq = [sb.alloc([128, 512], BF16, "sq") for _ in range(2)]
        sd = [sb.alloc([128, 512], F32, "sd") for _ in range(2)]
        qo = [sb.alloc([128, 512], BF16, "qo") for _ in range(3)]
        vo = [sb.alloc([128, D], BF16, "vo") for _ in range(2)]
        bank = cx.bank
        psT = [bank[0][:].bitcast(BF16).rearrange("p (k n) -> p k n", k=KC),
               bank[1][:].bitcast(BF16).rearrange("p (k n) -> p k n", k=KC)]
        psQ = [bank[2], bank[3]]
        psSS = [bank[4], bank[5]]
        psV = [bank[6], bank[7]]
        ntile = cx.ntok // 512
        rec.op("sp", lambda e: e.dma_start(out=gT[:], in_=p["norm"].rearrange("(k p) -> p k", p=128),
                                           allow_slow_non_contiguous=True), writes=[("gT", t)], dma=True)
        for hh in range(2):
            rec.op("sp", lambda e, hh=hh: e.dma_start(out=gqk[hh * 64:(hh + 1) * 64, 0:1], in_=p["q_norm"].rearrange("(d o) -> d o", o=1)),
                   writes=[("gqk", 0, hh)], dma=True)
            rec.op("sp", lambda e, hh=hh: e.dma_start(out=gqk[hh * 64:(hh + 1) * 64, 1:2], in_=p["k_norm"].rearrange("(d o) -> d o", o=1)),
                   writes=[("gqk", 1, hh)], dma=True)
        rec.op("dve", lambda e: e.tensor_scalar(out=gqk[:, 0:1], in0=gqk[:, 0:1], scalar1=float(HD) ** -0.5, scalar2=None, op0=ALU.mult),
               reads=[("gqk", 0, 0), ("gqk", 0, 1)], writes=[("gqk", 0, 0), ("gqk", 0, 1)])
        cnt = [0]
        for ti in range(ntile):
            tb = ti % 2
            for b in range(4):
                g = ti * 4 + b
                s3, s2 = g % 3, g % 2
                rms_prep(cx, rec, self.src[g * 128:(g + 1) * 128, :], xin[s3], ss[s3], rstd[s3], None, s3, [("X", self.src_n, g)])
                rec.op("dve", lambda e, s3=s3, s2=s2: e.tensor_scalar(out=hn[s2][:], in0=xin[s3][:], scalar1=rstd[s3][:, 0:1],
                                                                      scalar2=None, op0=ALU.mult),
                       reads=[("xin", s3), ("rstd", s3)], writes=[("hn", s2)])
                for kc in range(KC):
                    rec.op("pe", lambda e, kc=kc, s2=s2: e.transpose(out=psT[s2][:, kc, :], in_=hn[s2][:, kc * 128:(kc + 1) * 128],
                                                                    identity=cx.ident[:]),
                           reads=[("hn", s2), ("ident",)], writes=[("psT", s2)])
                rec.op("dve", lambda e, s2=s2, b=b, tb=tb: e.tensor_tensor(
                    out=hT[tb][:, :, b * 128:(b + 1) * 128], in0=psT[s2],
                    in1=gT[:].unsqueeze(2).broadcast_to([128, KC, 128]), op=ALU.mult),
                    reads=[("psT", s2), ("gT", t)], writes=[("hT", tb, b)])
            hts = [("hT", tb, b) for b in range(4)]
            for c in range(16):
                k = c % 2
                for kc in range(KC):
                    rec.op("pe", lambda e, c=c, kc=kc, k=k, tb=tb: e.matmul(psQ[k][:], lhsT=self.Win[:, kc, c * 128:(c + 1) * 128],
                                                                    rhs=hT[tb][:, kc, :], start=(kc == 0), stop=(kc == KC - 1)),
                           reads=hts + [("aWin", t, kc)], writes=[("psQ", k)], arena_reads=[0])
                rec.op("act", lambda e, k=k: e.activation(out=sq[k][:], in_=psQ[k][:], func=AF.Square),
                       reads=[("psQ", k)], writes=[("sq", k)])
                rec.op("pe", lambda e, k=k: e.matmul(psSS[k][:], lhsT=cx.blk64[:], rhs=sq[k][:], start=True, stop=True),
                       reads=[("sq", k), ("blk64",)], writes=[("psSS", k)])
                rec.op("act", lambda e, k=k: e.activation(out=sd[k][:], in_=psSS[k][:], func=AF.Sqrt, bias=RMS_EPS, scale=1.0 / HD),
                       reads=[("psSS", k)], writes=[("sd", k)])
                rec.op("dve", lambda e, k=k: e.reciprocal(out=sd[k][:], in_=sd[k][:]), reads=[("sd", k)], writes=[("sd", k)])
                q3 = cnt[0] % 3
                cnt[0] += 1
                col = 0 if c < 8 else 1
                rec.op("dve", lambda e, k=k, q3=q3, col=col: e.scalar_tensor_tensor(
                    out=qo[q3][:], in0=psQ[k][:], scalar=gqk[:, col:col + 1], in1=sd[k][:], op0=ALU.mult, op1=ALU.mult),
                    reads=[("psQ", k), ("sd", k), ("gqk", col, 0), ("gqk", col, 1)], writes=[("qo", q3)])
                dstT = (self.QT if c < 8 else self.KT)[c % 8]
                rec.op("sp", lambda e, q3=q3, dstT=dstT, ti=ti: e.dma_start(out=dstT[:, ti * 512:(ti + 1) * 512], in_=qo[q3][:]),
                       reads=[("qo", q3)], writes=[("QK_dram", c, ti)], dma=True)
            for b in range(4):
                g = ti * 4 + b
                v2 = g % 2
                for hf in range(2):
                    for kc in range(KC):
                        rec.op("pe", lambda e, hf=hf, kc=kc, b=b, tb=tb: e.matmul(
                            psV[hf][:], lhsT=hT[tb][:, kc, b * 128:(b + 1) * 128],
                            rhs=self.Win[:, kc, 2048 + hf * 512:2048 + (hf + 1) * 512], start=(kc == 0), stop=(kc == KC - 1)),
                            reads=[("hT", tb, b), ("aWin", t, kc)], writes=[("psV", hf)], arena_reads=[0])
                    rec.op("act", lambda e, hf=hf, v2=v2: e.copy(out=vo[v2][:, hf * 512:(hf + 1) * 512], in_=psV[hf][:]),
                           reads=[("psV", hf)], writes=[("vo", v2, hf)])
                rec.op("sp", lambda e, v2=v2, g=g: e.dma_start(out=self.V[g * 128:(g + 1) * 128, :], in_=vo[v2][:]),
                       reads=[("vo", v2, 0), ("vo", v2, 1)], writes=[("V_dram", g)], dma=True)

    def core(self):
        cx, rec, t, p = self.cx, self.cx.rec, self.tag, self.p
        sb = cx.sb
        S = cx.ntok
        nkb = S // 128
        ntile = S // 512
        sb.reset(cx.work0)
        lv = sb.alloc([128, 4, HD], F32, "lv")
        lt = sb.alloc([128, 8], F32, "lt")
        gsub = sb.alloc([128, 1], F32, "gsub")
        KTs = [sb.alloc([128, S], BF16, "KTs")] * 2
        Vs = [sb.alloc([128, nkb, 128], BF16, "Vs")] * 2
        QTs = [sb.alloc([128, 512], BF16, "QTs") for _ in range(2)]
        pT = [[sb.alloc([128, 512], BF16, "pT") for _ in range(2)] for _ in range(3)]
        rl = [sb.alloc([128, 512], F32, "rl") for _ in range(2)]
        oc = [sb.alloc([128, 512], F32, "oc") for _ in range(2)]
        osq = sb.alloc([128, 512], BF16, "osq")
        osd = sb.alloc([128, 512], F32, "osd")
        on = [sb.alloc([128, 512], BF16, "on") for _ in range(2)]
        bank = cx.bank
        psS = [[bank[0], bank[1]], [bank[2], bank[3]]]
        psO = [bank[4], bank[5]]
        psL = [bank[6], bank[7]]
        for i, nm in enumerate(("lq1", "lk1", "lq2", "lk2")):
            rec.op("sp", lambda e, i=i, nm=nm: e.dma_start(out=lv[:, i, :], in_=p[nm].partition_broadcast(128)),
                   writes=[("lv", i)], dma=True)
        rec.op("sp", lambda e: e.dma_start(out=gsub[:], in_=p["subln"].rearrange("(d o) -> d o", o=1)), writes=[("gsub",)], dma=True)
        rec.op("dve", lambda e: e.tensor_tensor(out=lv[:, 0, :], in0=lv[:, 0, :], in1=lv[:, 1, :], op=ALU.mult),
               reads=[("lv", 0), ("lv", 1)], writes=[("lv", 0)])
        rec.op("dve", lambda e: e.tensor_tensor(out=lv[:, 2, :], in0=lv[:, 2, :], in1=lv[:, 3, :], op=ALU.mult),
               reads=[("lv", 2), ("lv", 3)], writes=[("lv", 2)])
        rec.op("dve", lambda e: e.reduce_sum(out=lt[:, 0:1], in_=lv[:, 0, :], axis=AX.X), reads=[("lv", 0)], writes=[("lt", 0)])
        rec.op("dve", lambda e: e.reduce_sum(out=lt[:, 1:2], in_=lv[:, 2, :], axis=AX.X), reads=[("lv", 2)], writes=[("lt", 1)])
        rec.op("act", lambda e: e.activation(out=lt[:, 2:4], in_=lt[:, 0:2], func=AF.Exp), reads=[("lt", 0), ("lt", 1)], writes=[("lt", 2)])
        rec.op("dve", lambda e: e.tensor_tensor(out=lt[:, 4:5], in0=lt[:, 3:4], in1=lt[:, 2:3], op=ALU.subtract),
               reads=[("lt", 2)], writes=[("lt", 4)])
        rec.op("dve", lambda e: e.tensor_scalar(out=lt[:, 5:6], in0=lt[:, 4:5], scalar1=-self.lam_init, scalar2=None, op0=ALU.add),
               reads=[("lt", 4)], writes=[("neglam",)])
        rec.op("dve", lambda e: e.tensor_scalar(out=gsub[:], in0=gsub[:], scalar1=1.0 - self.lam_init, scalar2=None, op0=ALU.mult),
               reads=[("gsub",)], writes=[("gsub",)])
        neglam = lt[:, 5:6]
        it = [0]
        for h in range(NH):
            hb = 0
            for q in range(4):
                cs = slice(q * (S // 4), (q + 1) * (S // 4))
                rec.op("sp", lambda e, hb=hb, h=h, cs=cs: e.dma_start(out=KTs[hb][:, cs], in_=self.KT[h][:, cs]),
                       reads=[("QK_dram", 8 + h, tt) for tt in range(ntile)], writes=[("KTs", hb, q)], dma=True)
                bs = slice(q * (nkb // 4), (q + 1) * (nkb // 4))
                rec.op("sp", lambda e, hb=hb, h=h, bs=bs: e.dma_start(
                    out=Vs[hb][:, bs, :],
                    in_=self.V.rearrange("(b k) f -> k b f", k=128)[:, bs, h * 128:(h + 1) * 128]),
                    reads=[("V_dram", g) for g in range(nkb)], writes=[("Vs", hb, q)], dma=True)
            kvs = [("KTs", hb, q) for q in range(4)] + [("Vs", hb, q) for q in range(4)]
            for ti in range(ntile):
                qb = (h * ntile + ti) % 2
                rec.op("sp", lambda e, qb=qb, h=h, ti=ti: e.dma_start(out=QTs[qb][:], in_=self.QT[h][:, ti * 512:(ti + 1) * 512]),
                       reads=[("QK_dram", h, ti)], writes=[("QTs", qb)], dma=True)
                last = 4 * ti + 3
                for kb in range(last + 1):
                    j0 = max(0, kb - 4 * ti)
                    q0 = j0 * 128
                    sb2 = it[0] % 2
                    p3 = it[0] % 3
                    it[0] += 1
                    for c in range(2):
                        rec.op("pe", lambda e, c=c, sb2=sb2, hb=hb, kb=kb, qb=qb, q0=q0: e.matmul(
                            psS[sb2][c][:, q0:512], lhsT=KTs[hb][c * 64:(c + 1) * 64, kb * 128:(kb + 1) * 128],
                            rhs=QTs[qb][c * 64:(c + 1) * 64, q0:512], start=True, stop=True),
                            reads=kvs + [("QTs", qb)], writes=[("psS", sb2, c)])
                    for c in range(2):
                        rec.op("act", lambda e, c=c, sb2=sb2, p3=p3, q0=q0: e.activation(
                            out=pT[p3][c][:, q0:512], in_=psS[sb2][c][:, q0:512], func=AF.Exp),
                            reads=[("psS", sb2, c)], writes=[("pT", p3, c)])
                        if kb >= 4 * ti:
                            rec.op("pool", lambda e, c=c, p3=p3, q0=q0: e.tensor_tensor(
                                out=pT[p3][c][:, q0:q0 + 128], in0=pT[p3][c][:, q0:q0 + 128], in1=cx.triu[:], op=ALU.mult),
                                reads=[("pT", p3, c), ("triu",)], writes=[("pT", p3, c)])
                    for c in range(2):
                        rec.op("pe", lambda e, c=c, hb=hb, kb=kb, p3=p3, q0=q0, last=last: e.matmul(
                            psO[c][:, q0:512], lhsT=Vs[hb][:, kb, :], rhs=pT[p3][c][:, q0:512],
                            start=(kb == 0), stop=(kb == last)),
                            reads=kvs + [("pT", p3, c)], writes=[("psO", c)])
                        rec.op("pe", lambda e, c=c, p3=p3, q0=q0, kb=kb, last=last: e.matmul(
                            psL[c][:, q0:512], lhsT=cx.ones[:], rhs=pT[p3][c][:, q0:512],
                            start=(kb == 0), stop=(kb == last)),
                            reads=[("pT", p3, c), ("ones",)], writes=[("psL", c)])
                for c in range(2):
                    rec.op("dve", lambda e, c=c: e.reciprocal(out=rl[c][:], in_=psL[c][:]), reads=[("psL", c)], writes=[("rl", c)])
                    rec.op("dve", lambda e, c=c: e.tensor_tensor(out=oc[c][:], in0=psO[c][:], in1=rl[c][:], op=ALU.mult),
                           reads=[("psO", c), ("rl", c)], writes=[("oc", c)])
                rec.op("dve", lambda e: e.scalar_tensor_tensor(out=oc[0][:], in0=oc[1][:], scalar=neglam, in1=oc[0][:],
                                                               op0=ALU.mult, op1=ALU.add),
                       reads=[("oc", 0), ("oc", 1), ("neglam",)], writes=[("oc", 0)])
                rec.op("act", lambda e: e.activation(out=osq[:], in_=oc[0][:], func=AF.Square), reads=[("oc", 0)], writes=[("osq",)])
                nsb = it[0] % 2
                rec.op("pe", lambda e, nsb=nsb: e.matmul(psS[nsb][0][:], lhsT=cx.ones[:], rhs=osq[:], start=True, stop=True),
                       reads=[("osq",), ("ones",)], writes=[("psS", nsb, 0)])
                rec.op("act", lambda e, nsb=nsb: e.activation(out=osd[:], in_=psS[nsb][0][:], func=AF.Sqrt, bias=RMS_EPS, scale=1.0 / 128),
                       reads=[("psS", nsb, 0)], writes=[("osd",)])
                rec.op("dve", lambda e: e.reciprocal(out=osd[:], in_=osd[:]), reads=[("osd",)], writes=[("osd",)])
                ob = (h * ntile + ti) % 2
                rec.op("dve", lambda e, ob=ob: e.scalar_tensor_tensor(out=on[ob][:], in0=oc[0][:], scalar=gsub[:, 0:1], in1=osd[:],
                                                                      op0=ALU.mult, op1=ALU.mult),
                       reads=[("oc", 0), ("osd",), ("gsub",)], writes=[("on", ob)])
                rec.op("sp", lambda e, ob=ob, h=h, ti=ti: e.dma_start(out=self.OT[h][:, ti * 512:(ti + 1) * 512], in_=on[ob][:]),
                       reads=[("on", ob)], writes=[("OT_dram", h, ti)], dma=True)

    def outp(self):
        cx, rec, t, p = self.cx, self.cx.rec, self.tag, self.p
        sb = cx.sb
        S = cx.ntok
        sb.reset(cx.work0)
        xr = [sb.alloc([128, D], F32, "xr") for _ in range(3)]
        oT = [sb.alloc([128, NH, 128], BF16, "oT") for _ in range(3)]
        bank = cx.bank
        psO = [[bank[0], bank[1]], [bank[2], bank[3]]]
        OTv = self.OT.rearrange("h e s -> e h s")
        for b in range(S // 128):
            s3, s2 = b % 3, b % 2
            ti = b // 4
            rec.op("sp", lambda e, s3=s3, b=b: e.dma_start(out=oT[s3][:], in_=OTv[:, :, b * 128:(b + 1) * 128]),
                   reads=[("OT_dram", h, ti) for h in range(NH)], writes=[("oT", s3)], dma=True)
            rec.op("sp", lambda e, s3=s3, b=b: e.dma_start(out=xr[s3][:], in_=self.src[b * 128:(b + 1) * 128, :]),
                   reads=[("X", self.src_n, b)], writes=[("xr", s3)], dma=True)
            for dh in range(2):
                for h in range(NH):
                    rec.op("pe", lambda e, dh=dh, h=h, s3=s3, s2=s2: e.matmul(
                        psO[s2][dh][:], lhsT=oT[s3][:, h, :], rhs=self.Wout[:, h, dh * 512:(dh + 1) * 512],
                        start=(h == 0), stop=(h == NH - 1)),
                        reads=[("oT", s3), ("aWout", t, h // 4)], writes=[("psO", s2, dh)], arena_reads=[1])
                rec.op("dve", lambda e, dh=dh, s3=s3, s2=s2: e.tensor_tensor(
                    out=xr[s3][:, dh * 512:(dh + 1) * 512], in0=psO[s2][dh][:], in1=xr[s3][:, dh * 512:(dh + 1) * 512], op=ALU.add),
                    reads=[("psO", s2, dh), ("xr", s3)], writes=[("xr", s3)])
            rec.op("sp", lambda e, s3=s3, b=b: e.dma_start(out=self.dst[b * 128:(b + 1) * 128, :], in_=xr[s3][:]),
                   reads=[("xr", s3)], writes=[("X", self.dst_n, b)], dma=True)


import math
import ml_dtypes

SEQ = 8192
DEPTH = 4
PARAM_SHAPES = dict(
    ffn1_norm=[4, 1024], ffn1_w_gate_up=[4, 1024, 5632], ffn1_w_down=[4, 2816, 1024], mix_norm=[4, 1024],
    ffn2_norm=[4, 1024], ffn2_w_gate_up=[4, 1024, 5632], ffn2_w_down=[4, 2816, 1024],
    attn_w_in=[2, 1024, 3072], attn_w_out=[2, 1024, 1024], attn_q_norm=[2, 64], attn_k_norm=[2, 64],
    attn_lambda_q1=[2, 64], attn_lambda_k1=[2, 64], attn_lambda_q2=[2, 64], attn_lambda_k2=[2, 64], attn_subln=[2, 128],
    gmlp_w_in=[2, 1024, 4096], gmlp_b_in=[2, 4096], gmlp_ln_g=[2, 2048], gmlp_ln_b=[2, 2048],
    gmlp_w_s=[2, 8, 128, 128], gmlp_b_s=[2, 8, 128], gmlp_w_out=[2, 2048, 1024], gmlp_b_out=[2, 1024])


def build_program(S=SEQ, depth=DEPTH):
    nc = bass.Bass("TRN2", target_bir_lowering=False)
    P = {k: nc.dram_tensor(k, shp, F32, kind="ExternalInput").ap() for k, shp in PARAM_SHAPES.items()}
    x = nc.dram_tensor("x", [S, D], F32, kind="ExternalInput").ap()
    ident = nc.dram_tensor("c_ident", [128, 128], BF16, kind="ExternalInput").ap()
    tril = nc.dram_tensor("c_tril", [128, 128], F32, kind="ExternalInput").ap()
    cb = nc.dram_tensor("c_cb", [3, 128, 128], BF16, kind="ExternalInput").ap()
    out = nc.dram_tensor("out", [S, D], F32, kind="ExternalOutput").ap()
    XA = nc.dram_tensor("XA", [S, D], F32).ap()
    XB = nc.dram_tensor("XB", [S, D], F32).ap()
    scr = (nc.dram_tensor("QT", [8, 128, S], BF16).ap(), nc.dram_tensor("KT", [8, 128, S], BF16).ap(),
           nc.dram_tensor("V", [S, D], BF16).ap(), nc.dram_tensor("OT", [8, 128, S], BF16).ap())
    cx = Ctx(nc, S)
    rec = cx.rec
    load_consts(cx, ident, tril, cb)
    bx, ba, bb, bo = ("xext", x), ("XA", XA), ("XB", XB), ("out", out)
    stages = []
    cur = bx
    for i in range(depth):
        j = i // 2
        last = (i == depth - 1)
        f10 = FFNHalf(cx, 0, P["ffn1_w_gate_up"][i], P["ffn1_w_down"][i], P["ffn1_norm"][i], 0, cur, cur, bb, f"f1a{i}")
        f11 = FFNHalf(cx, 1, P["ffn1_w_gate_up"][i], P["ffn1_w_down"][i], P["ffn1_norm"][i], 1, cur, bb, ba, f"f1b{i}")
        stages.append(("ffn", {0}, f10.load, f10.compute))
        stages.append(("ffn", {1}, f11.load, f11.compute))
        cur = ba
        if i % 2 == 0:
            lam_init = 0.8 - 0.6 * math.exp(-0.3 * i)
            at = Attn(cx, P["attn_w_in"][j], P["attn_w_out"][j], P["attn_q_norm"][j], P["attn_k_norm"][j],
                      P["attn_lambda_q1"][j], P["attn_lambda_k1"][j], P["attn_lambda_q2"][j], P["attn_lambda_k2"][j],
                      P["attn_subln"][j], lam_init, P["mix_norm"][i], ba, ba, scr, f"at{i}")
            stages.append(("aproj", {0, 1}, at.load, at.proj))
            stages.append(("acore", set(), None, at.core))
            stages.append(("aout", {1}, None, at.outp))
        else:
            gm = GMLP(cx, P["gmlp_w_in"][j], P["gmlp_b_in"][j], P["gmlp_ln_g"][j], P["gmlp_ln_b"][j], P["gmlp_w_s"][j],
                      P["gmlp_b_s"][j], P["gmlp_w_out"][j], P["gmlp_b_out"][j], P["mix_norm"][i], ba, ba, f"gm{i}")
            stages.append(("gmlp", {0, 1}, gm.load, gm.compute))
        f20 = FFNHalf(cx, 0, P["ffn2_w_gate_up"][i], P["ffn2_w_down"][i], P["ffn2_norm"][i], 0, ba, ba, bb, f"f2a{i}")
        f21 = FFNHalf(cx, 1, P["ffn2_w_gate_up"][i], P["ffn2_w_down"][i], P["ffn2_norm"][i], 1, ba, bb, bo if last else ba, f"f2b{i}")
        stages.append(("ffn", {0}, f20.load, f20.compute))
        stages.append(("ffn", {1}, f21.load, f21.compute))
    loaded = [False] * len(stages)

    def busy_after(k):
        b = set()
        for m in range(k, len(stages)):
            if loaded[m] and stages[m][2] is not None:
                b |= stages[m][1] if stages[m][0] != "aproj" else {0, 1}
        return b

    prev_kind = None
    for k, (kind, arenas, load, compute) in enumerate(stages):
        if load is not None and not loaded[k]:
            load()
            loaded[k] = True
        if load is None:
            loaded[k] = True
        if prev_kind is not None and not (prev_kind == "ffn" and kind == "ffn"):
            rec.fence()
        nk = k + 1
        while nk < len(stages) and stages[nk][2] is None:
            nk += 1
        if nk < len(stages) and not loaded[nk]:
            need = stages[nk][1]
            inuse = set(arenas)
            if kind == "aproj":
                inuse = {0, 1}
            elif kind == "acore":
                inuse = {1}
            elif kind == "aout":
                inuse = {1}
            if not (need & inuse):
                stages[nk][2]()
                loaded[nk] = True
        compute()
        prev_kind = kind
    rec.emit()
    return nc, cx


_CACHE = {}


def kernel(**inputs):
    x = np.ascontiguousarray(np.asarray(inputs["x"], dtype=np.float32))
    B = x.shape[0]
    if "nc" not in _CACHE:
        _CACHE["nc"] = build_program()[0]
    nc = _CACHE["nc"]
    bf = ml_dtypes.bfloat16
    blk = np.zeros((128, 128), np.float32)
    blk[:64, :64] = 1
    blk[64:, 64:] = 1
    consts = dict(c_ident=np.eye(128, dtype=np.float32).astype(bf), c_tril=np.tril(np.ones((128, 128), np.float32)),
                  c_cb=np.stack([np.triu(np.ones((128, 128), np.float32)), np.ones((128, 128), np.float32), blk]).astype(bf))
    params = {k: np.ascontiguousarray(np.asarray(inputs[k], dtype=np.float32)) for k in PARAM_SHAPES}
    in_maps = [dict(x=x[b], **params, **consts) for b in range(B)]
    res = run_bass_kernel_spmd(nc, in_maps, core_ids=list(range(B)))
    return np.stack([np.asarray(r["out"], dtype=np.float32) for r in res.results], axis=0)
```

```python
import contextlib
import numpy as np
import concourse.bass as bass
import concourse.mybir as mybir
from concourse.bass_utils import run_bass_kernel_spmd

F32 = mybir.dt.float32
BF16 = mybir.dt.bfloat16
AF = mybir.ActivationFunctionType
ALU = mybir.AluOpType
AX = mybir.AxisListType

ENGS = ["pe", "act", "dve", "pool", "sp"]
R_DMA = 8


class Rec:
    def __init__(self, nc):
        self.nc = nc
        self.ops = []
        self.last_w = {}
        self.readers = {}
        self.arena_rd = {}
        self.pending = {e: set() for e in ENGS}
        self.last_op = {e: None for e in ENGS}
        self.recent_dma = {e: [] for e in ENGS}

    def op(self, eng, fn, reads=(), writes=(), dma=False, arena_reads=(), arena_writes=()):
        i = len(self.ops)
        deps = set()
        for r in reads:
            if r in self.last_w:
                deps.add(self.last_w[r])
        for w in writes:
            if w in self.last_w:
                deps.add(self.last_w[w])
            for rd in self.readers.get(w, {}).values():
                deps.add(rd)
        for a in arena_writes:
            for rd in self.arena_rd.get(a, {}).values():
                deps.add(rd)
        for a in arena_reads:
            self.arena_rd.setdefault(a, {})[eng if not dma else ("d", i)] = i
        for r in reads:
            self.readers.setdefault(r, {})[eng if not dma else ("d", i)] = i
        for w in writes:
            self.last_w[w] = i
            self.readers[w] = {}
        if self.pending[eng]:
            deps |= self.pending[eng]
            self.pending[eng] = set()
        deps.discard(i)
        self.ops.append(dict(eng=eng, fn=fn, deps=deps, dma=dma, marked=False))
        self.last_op[eng] = i
        if dma:
            self.recent_dma[eng].append(i)
            self.recent_dma[eng] = self.recent_dma[eng][-R_DMA:]
        return i

    def fence(self):
        deps = set()
        for e in ENGS:
            if self.last_op[e] is not None:
                deps.add(self.last_op[e])
            deps |= set(self.recent_dma[e])
        for e in ENGS:
            self.pending[e] |= deps

    def emit(self, final_wait_all=True):
        nc = self.nc
        ops = self.ops
        for o in ops:
            real = []
            for d in sorted(o["deps"]):
                p = ops[d]
                if (not p["dma"]) and (not o["dma"]) and p["eng"] == o["eng"] == "pe":
                    continue
                real.append(d)
                p["marked"] = True
            o["real"] = real
        cnt = {e: 0 for e in ENGS}
        dcnt = {e: 0 for e in ENGS}
        for o in ops:
            e = o["eng"]
            if o["dma"]:
                n = dcnt[e]
                dcnt[e] += 1
                o["sem"] = ("d", e, n % R_DMA)
                o["val"] = 16 * (n // R_DMA + 1)
                o["throttle"] = (("d", e, n % R_DMA), 16 * (n // R_DMA)) if n >= R_DMA else None
            elif o["marked"]:
                cnt[e] += 1
                o["sem"] = ("c", e)
                o["val"] = cnt[e]
        per_eng = {e: [o for o in ops if o["eng"] == e] for e in ENGS}
        semkeys = [("c", e) for e in ENGS if cnt[e] > 0]
        for e in ENGS:
            for k in range(min(dcnt[e], R_DMA)):
                semkeys.append(("d", e, k))
        self.stats = dict(cnt=dict(cnt), dcnt=dict(dcnt), nops={e: len(per_eng[e]) for e in ENGS})
        with contextlib.ExitStack() as st:
            semh = {}
            for k in semkeys:
                semh[k] = st.enter_context(nc.semaphore("s_" + "_".join(str(x) for x in k)))
            block = st.enter_context(nc.Block())

            def mk(engname):
                def body(eng):
                    waited = {}

                    def do_wait(s, v):
                        if v <= 0 or waited.get(s, 0) >= v:
                            return
                        waited[s] = v
                        eng.wait_ge(semh[s], v)

                    for o in per_eng[engname]:
                        if o["dma"] and o["throttle"]:
                            do_wait(*o["throttle"])
                        for d in o["real"]:
                            do_wait(ops[d]["sem"], ops[d]["val"])
                        ins = o["fn"](eng)
                        if o["dma"]:
                            ins.then_inc(semh[o["sem"]], 16)
                        elif o["marked"]:
                            ins.then_inc(semh[o["sem"]], 1)
                    if engname == "sp" and final_wait_all:
                        for e in ENGS:
                            n = dcnt[e]
                            for k in range(min(n, R_DMA)):
                                tot = len(range(k, n, R_DMA))
                                do_wait(("d", e, k), 16 * tot)
                return body

            block.tensor(mk("pe"))
            block.scalar(mk("act"))
            block.vector(mk("dve"))
            block.gpsimd(mk("pool"))
            block.sync(mk("sp"))


class SB:
    def __init__(self, nc, limit=None):
        self.nc = nc
        self.base = (nc.SBUF_PARTITION_SIZE_BYTES - nc.sbuf_bytes_remaining + 63) // 64 * 64
        self.limit = nc.SBUF_PARTITION_SIZE_BYTES - self.base
        self.off = 0
        self.n = 0

    def alloc(self, shape, dtype, name=None):
        esz = 2 if dtype == BF16 else 4
        free = int(np.prod(shape[1:])) * esz
        off = (self.off + 63) // 64 * 64
        assert off + free <= self.limit, f"SBUF overflow {off}+{free}>{self.limit}"
        self.off = off + free
        self.n += 1
        nm = f"{name or 't'}_{self.n}"
        return self.nc.alloc_sbuf_tensor_at(nm, list(shape), dtype, offset=self.base + off)

    def mark(self):
        return self.off

    def reset(self, m):
        self.off = m


D = 1024
KC = D // 128
FF = 2816
FH = FF // 2
JH = FH // 128
ARENA_BYTES = 66 * 1024 + 2048
RMS_EPS = 1e-6


class Ctx:
    def __init__(self, nc, ntok):
        self.nc = nc
        self.ntok = ntok
        self.rec = Rec(nc)
        self.sb = SB(nc)
        self.ident = self.sb.alloc([128, 128], BF16, "ident")
        self.tril = self.sb.alloc([128, 128], F32, "tril")
        self.triu = self.sb.alloc([128, 128], BF16, "triu")
        self.ones = self.sb.alloc([128, 128], BF16, "ones")
        self.blk64 = self.sb.alloc([128, 128], BF16, "blk64")
        self.arena = [self.sb.alloc([128, ARENA_BYTES // 2], BF16, f"arena{a}") for a in range(2)]
        self.work0 = self.sb.mark()
        self.bank = [nc.alloc_psum_tensor(f"bank{i}", [128, 512], F32) for i in range(8)]
        self.bank16 = [self.bank[i][:].bitcast(BF16) for i in range(8)]

    def arena_view(self, a, off_elems, shape):
        n = int(np.prod(shape))
        ap = self.arena[a][:, off_elems:off_elems + n]
        if len(shape) == 2:
            ap = ap.rearrange("p (a b) -> p a b", a=shape[0])
        return ap


def load_consts(cx, ident_dram, tril_dram=None, cb=None):
    rec = cx.rec
    rec.op("sp", lambda e: e.dma_start(out=cx.ident[:], in_=ident_dram), writes=[("ident",)], dma=True)
    if cb is not None:
        rec.op("sp", lambda e: e.dma_start(out=cx.triu[:], in_=cb[0]), writes=[("triu",)], dma=True)
        rec.op("sp", lambda e: e.dma_start(out=cx.ones[:], in_=cb[1]), writes=[("ones",)], dma=True)
        rec.op("sp", lambda e: e.dma_start(out=cx.blk64[:], in_=cb[2]), writes=[("blk64",)], dma=True)
    if tril_dram is not None:
        rec.op("sp", lambda e: e.dma_start(out=cx.tril[:], in_=tril_dram), writes=[("tril",)], dma=True)


class FFNHalf:
    def __init__(self, cx, a, w_gu, w_down, norm, half, nsrc, res, dst, tag):
        self.cx, self.a, self.half, self.tag = cx, a, half, tag
        self.w_gu, self.w_down, self.norm = w_gu, w_down, norm
        (self.nsrc_n, self.nsrc), (self.res_n, self.res), (self.dst_n, self.dst) = nsrc, res, dst
        self.Wg = cx.arena_view(a, 0, [KC, FH])
        self.Wu = cx.arena_view(a, KC * FH, [KC, FH])
        self.Wd = cx.arena_view(a, 2 * KC * FH, [JH, D])
        self.gT = cx.arena[a][:, 2 * KC * FH + JH * D: 2 * KC * FH + JH * D + 2 * KC].bitcast(F32)

    def load(self):
        cx, rec, a, t = self.cx, self.cx.rec, self.a, self.tag
        c0 = self.half * FH
        gu = self.w_gu.rearrange("(k p) f -> p k f", p=128)
        for kp in range(KC // 2):
            ks = slice(2 * kp, 2 * kp + 2)
            rec.op("pool", lambda e, ks=ks: e.dma_start(out=self.Wg[:, ks, :], in_=gu[:, ks, c0:c0 + FH]),
                   writes=[("Wg", t, kp)], dma=True, arena_writes=[a])
            rec.op("pool", lambda e, ks=ks: e.dma_start(out=self.Wu[:, ks, :], in_=gu[:, ks, FF + c0:FF + c0 + FH]),
                   writes=[("Wu", t, kp)], dma=True, arena_writes=[a])
        wd = self.w_down.rearrange("(j p) d -> p j d", p=128)
        j0 = self.half * JH
        for q in range(3):
            js = slice(4 * q, min(4 * q + 4, JH))
            rec.op("pool", lambda e, js=js: e.dma_start(out=self.Wd[:, js, :], in_=wd[:, j0 + js.start:j0 + js.stop, :]),
                   writes=[("Wd", t, q)], dma=True, arena_writes=[a])
        rec.op("sp", lambda e: e.dma_start(out=self.gT, in_=self.norm.rearrange("(k p) -> p k", p=128),
                                           allow_slow_non_contiguous=True),
               writes=[("gT", t)], dma=True, arena_writes=[a])

    def compute(self):
        cx, rec, a, t = self.cx, self.cx.rec, self.a, self.tag
        nc = cx.nc
        sb = cx.sb
        sb.reset(cx.work0)
        xin = [sb.alloc([128, D], F32, "xin") for _ in range(3)]
        junk = sb.alloc([128, D], BF16, "junk")
        ss = [sb.alloc([128, 1], F32, "ss") for _ in range(3)]
        rstd = [sb.alloc([128, 1], F32, "rstd") for _ in range(3)]
        hn = [sb.alloc([128, D], BF16, "hn") for _ in range(4)]
        hT = [sb.alloc([128, KC, 512], BF16, "hT") for _ in range(2)]
        sg = [sb.alloc([128, 512], F32, "sg") for _ in range(2)]
        aT = sb.alloc([128, JH, 512], BF16, "aT")
        xr = [sb.alloc([128, D], F32, "xr") for _ in range(3)]
        psT = [cx.bank16[0].rearrange("p (k n) -> p k n", k=KC),
               cx.bank16[1].rearrange("p (k n) -> p k n", k=KC)]
        psG = [cx.bank[2], cx.bank[3]]
        psU = [cx.bank[4], cx.bank[5]]
        psD = [cx.bank[6], cx.bank[7]]
        ntile = cx.ntok // 512
        self.dbg = dict(hT=hT, aT=aT, hn=hn, rstd=rstd, ss=ss)

        def prep_nonpe(ti):
            for b in range(4):
                g = ti * 4 + b
                s = g % 3
                r0 = g * 128
                rec.op("sp", lambda e, s=s, r0=r0: e.dma_start(out=xin[s][:], in_=self.nsrc[r0:r0 + 128, :]),
                       reads=[("X", self.nsrc_n, g)], writes=[("xin", s)], dma=True)
                rec.op("act", lambda e, s=s: e.activation(out=junk[:], in_=xin[s][:], func=AF.Square,
                                                          accum_out=ss[s][:, 0:1]),
                       reads=[("xin", s)], writes=[("ss", s)])
                rec.op("act", lambda e, s=s: e.activation(out=ss[s][:, 0:1], in_=ss[s][:, 0:1], func=AF.Sqrt,
                                                          bias=RMS_EPS, scale=1.0 / D),
                       reads=[("ss", s)], writes=[("ss", s)])
                rec.op("dve", lambda e, s=s: e.reciprocal(out=rstd[s][:], in_=ss[s][:]),
                       reads=[("ss", s)], writes=[("rstd", s)])
                h = g % 4
                rec.op("dve", lambda e, s=s, h=h: e.tensor_scalar(out=hn[h][:], in0=xin[s][:], scalar1=rstd[s][:, 0:1],
                                                                  scalar2=None, op0=ALU.mult),
                       reads=[("xin", s), ("rstd", s)], writes=[("hn", h)])

        def prep_pe(ti):
            tb = ti % 2
            for b in range(4):
                g = ti * 4 + b
                h = g % 4
                p = g % 2
                for kc in range(KC):
                    rec.op("pe", lambda e, h=h, p=p, kc=kc: e.transpose(out=psT[p][:, kc, :],
                                                                       in_=hn[h][:, kc * 128:(kc + 1) * 128],
                                                                       identity=cx.ident[:]),
                           reads=[("hn", h), ("ident",)], writes=[("psT", p)])
                rec.op("dve", lambda e, p=p, tb=tb, b=b: e.tensor_tensor(
                    out=hT[tb][:, :, b * 128:(b + 1) * 128], in0=psT[p],
                    in1=self.gT.unsqueeze(2).broadcast_to([128, KC, 128]), op=ALU.mult),
                    reads=[("psT", p), ("gT", t)], writes=[("hT", tb, b)], arena_reads=[a])

        def gate_up(ti):
            tb = ti % 2
            for j in range(JH):
                s = j % 2
                for (W, ps, nm) in ((self.Wg, psG, "Wg"), (self.Wu, psU, "Wu")):
                    for kc in range(KC):
                        rec.op("pe", lambda e, W=W, ps=ps, kc=kc, j=j, s=s: e.matmul(
                            ps[s][:], lhsT=W[:, kc, j * 128:(j + 1) * 128], rhs=hT[tb][:, kc, :],
                            start=(kc == 0), stop=(kc == KC - 1)),
                            reads=[(nm, t, kc // 2)] + [("hT", tb, b) for b in range(4)],
                            writes=[("ps" + nm, s)], arena_reads=[a])
                rec.op("act", lambda e, s=s: e.activation(out=sg[s][:], in_=psG[s][:], func=AF.Silu),
                       reads=[("psWg", s)], writes=[("sg", s)])
                rec.op("dve", lambda e, s=s, j=j: e.tensor_tensor(out=aT[:, j, :], in0=sg[s][:], in1=psU[s][:],
                                                                  op=ALU.mult),
                       reads=[("sg", s), ("psWu", s)], writes=[("aT", j)])

        def down(ti):
            for b in range(4):
                g = ti * 4 + b
                s = g % 3
                r0 = g * 128
                rec.op("sp", lambda e, s=s, r0=r0: e.dma_start(out=xr[s][:], in_=self.res[r0:r0 + 128, :]),
                       reads=[("X", self.res_n, g)], writes=[("xr", s), ("xo", s, 0), ("xo", s, 1)], dma=True)
                for dh in range(2):
                    for j in range(JH):
                        rec.op("pe", lambda e, dh=dh, j=j, b=b: e.matmul(
                            psD[dh][:], lhsT=aT[:, j, b * 128:(b + 1) * 128], rhs=self.Wd[:, j, dh * 512:(dh + 1) * 512],
                            start=(j == 0), stop=(j == JH - 1)),
                            reads=[("aT", j), ("Wd", t, j // 4)], writes=[("psD", dh)], arena_reads=[a])
                    rec.op("dve", lambda e, dh=dh, s=s: e.scalar_tensor_tensor(
                        out=xr[s][:, dh * 512:(dh + 1) * 512], in0=psD[dh][:], scalar=0.5,
                        in1=xr[s][:, dh * 512:(dh + 1) * 512], op0=ALU.mult, op1=ALU.add),
                        reads=[("psD", dh), ("xr", s)], writes=[("xo", s, dh)])
                rec.op("sp", lambda e, s=s, r0=r0: e.dma_start(out=self.dst[r0:r0 + 128, :], in_=xr[s][:]),
                       reads=[("xo", s, 0), ("xo", s, 1)], writes=[("X", self.dst_n, g)], dma=True)

        prep_nonpe(0)
        prep_pe(0)
        for ti in range(ntile):
            if ti + 1 < ntile:
                prep_nonpe(ti + 1)
            gate_up(ti)
            if ti + 1 < ntile:
                prep_pe(ti + 1)
            down(ti)


def rms_prep(cx, rec, src_rows, xin, ss, rstd, hn, key, dres=()):
    rec.op("sp", lambda e: e.dma_start(out=xin[:], in_=src_rows), reads=list(dres), writes=[("xin", key)], dma=True)
    rec.op("act", lambda e: e.activation(out=cx.junk[:, 0:D], in_=xin[:], func=AF.Square, accum_out=ss[:, 0:1]),
           reads=[("xin", key)], writes=[("ss", key)])
    rec.op("act", lambda e: e.activation(out=ss[:, 0:1], in_=ss[:, 0:1], func=AF.Sqrt, bias=RMS_EPS, scale=1.0 / D),
           reads=[("ss", key)], writes=[("ss", key)])
    rec.op("dve", lambda e: e.reciprocal(out=rstd[:], in_=ss[:]), reads=[("ss", key)], writes=[("rstd", key)])


GH = 2048
LN_EPS = 1e-5


class GMLP:
    def __init__(self, cx, w_in, b_in, ln_g, ln_b, w_s, b_s, w_out, b_out, norm, src, dst, tag):
        self.cx, self.tag = cx, tag
        self.p = dict(w_in=w_in, b_in=b_in, ln_g=ln_g, ln_b=ln_b, w_s=w_s, b_s=b_s, w_out=w_out, b_out=b_out, norm=norm)
        (self.src_n, self.src), (self.dst_n, self.dst) = src, dst
        A0, A1 = cx.arena[0], cx.arena[1]
        self.Win = A0[:, 0:KC * 4096].rearrange("p (k f) -> p k f", k=KC)
        self.bout = A0[:, KC * 4096:KC * 4096 + 2 * D].bitcast(F32)
        o = 0
        self.Wout = A1[:, o:o + 16 * D].rearrange("p (k f) -> p k f", k=16); o += 16 * D
        self.wsT = A1[:, o:o + 8 * 128].rearrange("p (g t) -> p g t", g=8); o += 8 * 128
        self.bin = A1[:, o:o + 2 * 4096].bitcast(F32); o += 2 * 4096
        self.lng = A1[:, o:o + 2 * GH].bitcast(F32); o += 2 * GH
        self.lnb = A1[:, o:o + 2 * GH].bitcast(F32); o += 2 * GH
        assert o <= ARENA_BYTES // 2

    def load(self):
        cx, rec, t, p = self.cx, self.cx.rec, self.tag, self.p
        win = p["w_in"].rearrange("(k p) f -> p k f", p=128)
        for kc in range(KC):
            rec.op("pool", lambda e, kc=kc: e.dma_start(out=self.Win[:, kc, :], in_=win[:, kc, :]),
                   writes=[("gWin", t, kc)], dma=True, arena_writes=[0])
        wout = p["w_out"].rearrange("(k p) f -> p k f", p=128)
        for q in range(4):
            rec.op("pool", lambda e, q=q: e.dma_start(out=self.Wout[:, 4 * q:4 * q + 4, :], in_=wout[:, 4 * q:4 * q + 4, :]),
                   writes=[("gWout", t, q)], dma=True, arena_writes=[1])
        for (nm, dst_ap, src_ap, ar) in (("bin", self.bin, p["b_in"], 1), ("lng", self.lng, p["ln_g"], 1),
                                         ("lnb", self.lnb, p["ln_b"], 1), ("bout", self.bout, p["b_out"], 0)):
            rec.op("sp", lambda e, dst_ap=dst_ap, src_ap=src_ap: e.dma_start(out=dst_ap, in_=src_ap.partition_broadcast(128)),
                   writes=[("g" + nm, t)], dma=True, arena_writes=[ar])

    def compute(self):
        cx, rec, t, p = self.cx, self.cx.rec, self.tag, self.p
        sb = cx.sb
        sb.reset(cx.work0)
        cx.junk = sb.alloc([128, GH], BF16, "junk")
        gT = sb.alloc([128, KC], F32, "gT")
        bsT = sb.alloc([128, 8], F32, "bsT")
        wtmp = [sb.alloc([128, 128], F32, "wtmp") for _ in range(2)]
        wtmpb = [sb.alloc([128, 128], BF16, "wtmpb") for _ in range(2)]
        xin = [sb.alloc([128, D], F32, "xin") for _ in range(3)]
        ss = [sb.alloc([128, 1], F32, "ss") for _ in range(3)]
        rstd = [sb.alloc([128, 1], F32, "rstd") for _ in range(3)]
        hn = [sb.alloc([128, D], BF16, "hn") for _ in range(2)]
        hT = [sb.alloc([128, KC, 128], BF16, "hT") for _ in range(2)]
        zb = [sb.alloc([128, 512], F32, "zb") for _ in range(2)]
        u = sb.alloc([128, GH], F32, "u")
        v = sb.alloc([128, GH], F32, "v")
        vsum = sb.alloc([128, 4], F32, "vsum")
        st = sb.alloc([128, 4], F32, "st")
        vn = sb.alloc([128, GH], F32, "vn")
        vnb = sb.alloc([128, GH], BF16, "vnb")
        gated = sb.alloc([128, GH], BF16, "gated")
        gatedT = sb.alloc([128, 16, 128], BF16, "gatedT")
        bank = cx.bank
        psT = cx.bank16[0].rearrange("p (k n) -> p k n", k=KC)
        psZ = [bank[1], bank[2]]
        psS = [bank[3], bank[4]]
        psGT = cx.bank16[5].rearrange("p (k n) -> p k n", k=8)
        psO = [bank[6], bank[7]]
        nblk = cx.ntok // 128

        rec.op("sp", lambda e: e.dma_start(out=gT[:], in_=p["norm"].rearrange("(k p) -> p k", p=128),
                                           allow_slow_non_contiguous=True), writes=[("gT", t)], dma=True)
        rec.op("sp", lambda e: e.dma_start(out=bsT[:], in_=p["b_s"].rearrange("g t -> t g"),
                                           allow_slow_non_contiguous=True), writes=[("bsT", t)], dma=True)
        for g in range(8):
            k = g % 2
            rec.op("sp", lambda e, g=g, k=k: e.dma_start(out=wtmp[k][:], in_=p["w_s"][g]), writes=[("wtmp", k)], dma=True)
            rec.op("dve", lambda e, k=k: e.tensor_tensor(out=wtmpb[k][:], in0=wtmp[k][:], in1=cx.tril[:], op=ALU.mult),
                   reads=[("wtmp", k), ("tril",)], writes=[("wtmpb", k)])
            rec.op("pe", lambda e, k=k: e.transpose(out=psT[:, 0, :], in_=wtmpb[k][:], identity=cx.ident[:]),
                   reads=[("wtmpb", k), ("ident",)], writes=[("psT", "h")])
            rec.op("act", lambda e, g=g, k=k: e.copy(out=self.wsT[:, g, :], in_=psT[:, 0, :]),
                   reads=[("psT", "h")], writes=[("wsT", t, g)], arena_writes=[1])

        def A(b):
            s3, s2 = b % 3, b % 2
            rms_prep(cx, rec, self.src[b * 128:(b + 1) * 128, :], xin[s3], ss[s3], rstd[s3], None, s3, [("X", self.src_n, b)])
            rec.op("dve", lambda e: e.tensor_scalar(out=hn[s2][:], in0=xin[s3][:], scalar1=rstd[s3][:, 0:1],
                                                    scalar2=None, op0=ALU.mult),
                   reads=[("xin", s3), ("rstd", s3)], writes=[("hn", s2)])
            for kc in range(KC):
                rec.op("pe", lambda e, kc=kc: e.transpose(out=psT[:, kc, :], in_=hn[s2][:, kc * 128:(kc + 1) * 128],
                                                          identity=cx.ident[:]),
                       reads=[("hn", s2), ("ident",)], writes=[("psT", "h")])
            rec.op("dve", lambda e: e.tensor_tensor(out=hT[s2][:], in0=psT,
                                                    in1=gT[:].unsqueeze(2).broadcast_to([128, KC, 128]), op=ALU.mult),
                   reads=[("psT", "h"), ("gT", t)], writes=[("hT", s2)])
            rec.op("pool", lambda e: e.tensor_tensor(out=xin[s3][:], in0=xin[s3][:], in1=self.bout, op=ALU.add),
                   reads=[("xin", s3), ("hn", s2), ("gbout", t)], writes=[("xin", s3)], arena_reads=[0])

        def Bz(b, slabs):
            s2 = b % 2
            for c in slabs:
                k = c % 2
                for kc in range(KC):
                    rec.op("pe", lambda e, c=c, kc=kc, k=k: e.matmul(psZ[k][:], lhsT=hT[s2][:, kc, :],
                                                                    rhs=self.Win[:, kc, c * 512:(c + 1) * 512],
                                                                    start=(kc == 0), stop=(kc == KC - 1)),
                           reads=[("hT", s2), ("gWin", t, kc)], writes=[("psZ", k)], arena_reads=[0])
                rec.op("dve", lambda e, c=c, k=k: e.tensor_tensor(out=zb[k][:], in0=psZ[k][:],
                                                                  in1=self.bin[:, c * 512:(c + 1) * 512], op=ALU.add),
                       reads=[("psZ", k), ("gbin", t)], writes=[("zb", k)], arena_reads=[1])
                if c < 4:
                    rec.op("act", lambda e, c=c, k=k: e.activation(out=u[:, c * 512:(c + 1) * 512], in_=zb[k][:], func=AF.Gelu),
                           reads=[("zb", k)], writes=[("u", c)])
                else:
                    rec.op("act", lambda e, c=c, k=k: e.activation(out=v[:, (c - 4) * 512:(c - 3) * 512], in_=zb[k][:],
                                                                   func=AF.Gelu, accum_out=vsum[:, c - 4:c - 3]),
                           reads=[("zb", k)], writes=[("v", c - 4), ("vsum", c - 4)])

        def C(b):
            allv = [("v", i) for i in range(4)]
            rec.op("dve", lambda e: e.reduce_sum(out=st[:, 0:1], in_=vsum[:], axis=AX.X),
                   reads=[("vsum", i) for i in range(4)], writes=[("st", 0)])
            rec.op("dve", lambda e: e.tensor_scalar(out=st[:, 1:2], in0=st[:, 0:1], scalar1=-1.0 / GH, scalar2=None, op0=ALU.mult),
                   reads=[("st", 0)], writes=[("st", 1)])
            rec.op("act", lambda e: e.activation(out=cx.junk[:], in_=v[:], func=AF.Square, bias=st[:, 1:2],
                                                 accum_out=st[:, 2:3]),
                   reads=allv + [("st", 1)], writes=[("st", 2)])
            rec.op("act", lambda e: e.activation(out=st[:, 2:3], in_=st[:, 2:3], func=AF.Sqrt, bias=LN_EPS, scale=1.0 / GH),
                   reads=[("st", 2)], writes=[("st", 2)])
            rec.op("dve", lambda e: e.reciprocal(out=st[:, 3:4], in_=st[:, 2:3]), reads=[("st", 2)], writes=[("st", 3)])
            rec.op("dve", lambda e: e.tensor_scalar(out=vn[:], in0=v[:], scalar1=st[:, 1:2], scalar2=st[:, 3:4],
                                                    op0=ALU.add, op1=ALU.mult),
                   reads=allv + [("st", 1), ("st", 3)], writes=[("vn",)])
            rec.op("pool", lambda e: e.tensor_tensor(out=vn[:], in0=vn[:], in1=self.lng, op=ALU.mult),
                   reads=[("vn",), ("glng", t)], writes=[("vn",)], arena_reads=[1])
            rec.op("pool", lambda e: e.tensor_tensor(out=vnb[:], in0=vn[:], in1=self.lnb, op=ALU.add),
                   reads=[("vn",), ("glnb", t)], writes=[("vnb",)], arena_reads=[1])

        def Dg(b):
            for pr in range(4):
                k = pr % 2
                for g in (2 * pr, 2 * pr + 1):
                    rec.op("pe", lambda e, g=g, k=k: e.matmul(psS[k][:, (g % 2) * 256:(g % 2 + 1) * 256], lhsT=self.wsT[:, g, :],
                                                              rhs=vnb[:, g * 256:(g + 1) * 256], start=True, stop=True),
                           reads=[("vnb",), ("wsT", t, g)], writes=[("psS", k)], arena_reads=[1])
                for g in (2 * pr, 2 * pr + 1):
                    rec.op("dve", lambda e, g=g, k=k: e.scalar_tensor_tensor(
                        out=gated[:, g * 256:(g + 1) * 256], in0=psS[k][:, (g % 2) * 256:(g % 2 + 1) * 256],
                        scalar=bsT[:, g:g + 1], in1=u[:, g * 256:(g + 1) * 256], op0=ALU.add, op1=ALU.mult),
                        reads=[("psS", k), ("bsT", t), ("u", g // 2)], writes=[("gated", g // 4, g % 4)])

        def E(b):
            for hh in range(2):
                for i in range(8):
                    fc = hh * 8 + i
                    rec.op("pe", lambda e, fc=fc, i=i: e.transpose(out=psGT[:, i, :], in_=gated[:, fc * 128:(fc + 1) * 128],
                                                                  identity=cx.ident[:]),
                           reads=[("gated", fc // 8, j) for j in range(4)] + [("ident",)], writes=[("psGT",)])
                rec.op("act", lambda e, hh=hh: e.copy(out=gatedT[:, hh * 8:(hh + 1) * 8, :], in_=psGT),
                       reads=[("psGT",)], writes=[("gatedT", hh)])

        def Fo(b):
            s3 = b % 3
            for dh in range(2):
                for fc in range(16):
                    rec.op("pe", lambda e, dh=dh, fc=fc: e.matmul(psO[dh][:], lhsT=gatedT[:, fc, :],
                                                                  rhs=self.Wout[:, fc, dh * 512:(dh + 1) * 512],
                                                                  start=(fc == 0), stop=(fc == 15)),
                           reads=[("gatedT", fc // 8), ("gWout", t, fc // 4)], writes=[("psO", dh)], arena_reads=[1])
                rec.op("dve", lambda e, dh=dh: e.tensor_tensor(out=xin[s3][:, dh * 512:(dh + 1) * 512], in0=psO[dh][:],
                                                               in1=xin[s3][:, dh * 512:(dh + 1) * 512], op=ALU.add),
                       reads=[("psO", dh), ("xin", s3)], writes=[("xin", s3)])
            rec.op("sp", lambda e: e.dma_start(out=self.dst[b * 128:(b + 1) * 128, :], in_=xin[s3][:]),
                   reads=[("xin", s3)], writes=[("X", self.dst_n, b)], dma=True)

        A(0)
        Bz(0, [4, 5, 6, 7, 0, 1, 2, 3])
        for b in range(nblk):
            nxt = b + 1 < nblk
            if nxt:
                A(b + 1)
            C(b)
            if nxt:
                Bz(b + 1, [4, 5, 6, 7])
            Dg(b)
            if nxt:
                Bz(b + 1, [0, 1, 2, 3])
            E(b)
            Fo(b)


NH = 8
HD = 64


class Attn:
    def __init__(self, cx, w_in, w_out, q_norm, k_norm, lq1, lk1, lq2, lk2, subln, lam_init, norm, src, dst, scr, tag):
        self.cx, self.tag = cx, tag
        self.p = dict(w_in=w_in, w_out=w_out, q_norm=q_norm, k_norm=k_norm, lq1=lq1, lk1=lk1, lq2=lq2, lk2=lk2,
                      subln=subln, norm=norm)
        self.lam_init = float(lam_init)
        (self.src_n, self.src), (self.dst_n, self.dst) = src, dst
        self.QT, self.KT, self.V, self.OT = scr
        A0, A1 = cx.arena[0], cx.arena[1]
        self.Win = A0[:, 0:KC * 3072].rearrange("p (k f) -> p k f", k=KC)
        self.Wout = A1[:, 0:NH * D].rearrange("p (k f) -> p k f", k=NH)

    def load(self):
        cx, rec, t, p = self.cx, self.cx.rec, self.tag, self.p
        win = p["w_in"].rearrange("(k p) f -> p k f", p=128)
        for kc in range(KC):
            rec.op("pool", lambda e, kc=kc: e.dma_start(out=self.Win[:, kc, :], in_=win[:, kc, :]),
                   writes=[("aWin", t, kc)], dma=True, arena_writes=[0])
        wout = p["w_out"].rearrange("(k p) f -> p k f", p=128)
        for q in range(2):
            rec.op("pool", lambda e, q=q: e.dma_start(out=self.Wout[:, 4 * q:4 * q + 4, :], in_=wout[:, 4 * q:4 * q + 4, :]),
                   writes=[("aWout", t, q)], dma=True, arena_writes=[1])

    def proj(self):
        cx, rec, t, p = self.cx, self.cx.rec, self.tag, self.p
        sb = cx.sb
        sb.reset(cx.work0)
        cx.junk = sb.alloc([128, D], BF16, "junk")
        gT = sb.alloc([128, KC], F32, "gT")
        gqk = sb.alloc([128, 2], F32, "gqk")
        xin = [sb.alloc([128, D], F32, "xin") for _ in range(3)]
        ss = [sb.alloc([128, 1], F32, "ss") for _ in range(3)]
        rstd = [sb.alloc([128, 1], F32, "rstd") for _ in range(3)]
        hn = [sb.alloc([128, D], BF16, "hn") for _ in range(2)]
        hT = [sb.alloc([128, KC, 512], BF16, "hT") for _ in range(2)]
        sq = [sb.alloc([128, 512], BF16, "sq") for _ in range(2)]
        sd = [sb.alloc([128, 512], F32, "sd") for _ in range(2)]
        qo = [sb.alloc([128, 512], BF16, "qo") for _ in range(3)]
        vo = [sb.alloc([128, D], BF16, "vo") for _ in range(2)]
        bank = cx.bank
        psT = [cx.bank16[0].rearrange("p (k n) -> p k n", k=KC),
               cx.bank16[1].rearrange("p (k n) -> p k n", k=KC)]
        psQ = [bank[2], bank[3]]
        psSS = [bank[4], bank[5]]
        psV = [bank[6], bank[7]]
        ntile = cx.ntok // 512
        rec.op("sp", lambda e: e.dma_start(out=gT[:], in_=p["norm"].rearrange("(k p) -> p k", p=128),
                                           allow_slow_non_contiguous=True), writes=[("gT", t)], dma=True)
        for hh in range(2):
            rec.op("sp", lambda e, hh=hh: e.dma_start(out=gqk[hh * 64:(hh + 1) * 64, 0:1], in_=p["q_norm"].rearrange("(d o) -> d o", o=1)),
                   writes=[("gqk", 0, hh)], dma=True)
            rec.op("sp", lambda e, hh=hh: e.dma_start(out=gqk[hh * 64:(hh + 1) * 64, 1:2], in_=p["k_norm"].rearrange("(d o) -> d o", o=1)),
                   writes=[("gqk", 1, hh)], dma=True)
        rec.op("dve", lambda e: e.tensor_scalar(out=gqk[:, 0:1], in0=gqk[:, 0:1], scalar1=float(HD) ** -0.5, scalar2=None, op0=ALU.mult),
               reads=[("gqk", 0, 0), ("gqk", 0, 1)], writes=[("gqk", 0, 0), ("gqk", 0, 1)])
        cnt = [0]
        for ti in range(ntile):
            tb = ti % 2
            for b in range(4):
                g = ti * 4 + b
                s3, s2 = g % 3, g % 2
                rms_prep(cx, rec, self.src[g * 128:(g + 1) * 128, :], xin[s3], ss[s3], rstd[s3], None, s3, [("X", self.src_n, g)])
                rec.op("dve", lambda e, s3=s3, s2=s2: e.tensor_scalar(out=hn[s2][:], in0=xin[s3][:], scalar1=rstd[s3][:, 0:1],
                                                                      scalar2=None, op0=ALU.mult),
                       reads=[("xin", s3), ("rstd", s3)], writes=[("hn", s2)])
                for kc in range(KC):
                    rec.op("pe", lambda e, kc=kc, s2=s2: e.transpose(out=psT[s2][:, kc, :], in_=hn[s2][:, kc * 128:(kc + 1) * 128],
                                                                    identity=cx.ident[:]),
                           reads=[("hn", s2), ("ident",)], writes=[("psT", s2)])
                rec.op("dve", lambda e, s2=s2, b=b, tb=tb: e.tensor_tensor(
                    out=hT[tb][:, :, b * 128:(b + 1) * 128], in0=psT[s2],
                    in1=gT[:].unsqueeze(2).broadcast_to([128, KC, 128]), op=ALU.mult),
                    reads=[("psT", s2), ("gT", t)], writes=[("hT", tb, b)])
            hts = [("hT", tb, b) for b in range(4)]
            for c in range(16):
                k = c % 2
                for kc in range(KC):
                    rec.op("pe", lambda e, c=c, kc=kc, k=k, tb=tb: e.matmul(psQ[k][:], lhsT=self.Win[:, kc, c * 128:(c + 1) * 128],
                                                                    rhs=hT[tb][:, kc, :], start=(kc == 0), stop=(kc == KC - 1)),
                           reads=hts + [("aWin", t, kc)], writes=[("psQ", k)], arena_reads=[0])
                rec.op("act", lambda e, k=k: e.activation(out=sq[k][:], in_=psQ[k][:], func=AF.Square),
                       reads=[("psQ", k)], writes=[("sq", k)])
                rec.op("pe", lambda e, k=k: e.matmul(psSS[k][:], lhsT=cx.blk64[:], rhs=sq[k][:], start=True, stop=True),
                       reads=[("sq", k), ("blk64",)], writes=[("psSS", k)])
                rec.op("act", lambda e, k=k: e.activation(out=sd[k][:], in_=psSS[k][:], func=AF.Sqrt, bias=RMS_EPS, scale=1.0 / HD),
                       reads=[("psSS", k)], writes=[("sd", k)])
                rec.op("dve", lambda e, k=k: e.reciprocal(out=sd[k][:], in_=sd[k][:]), reads=[("sd", k)], writes=[("sd", k)])
                q3 = cnt[0] % 3
                cnt[0] += 1
                col = 0 if c < 8 else 1
                rec.op("dve", lambda e, k=k, q3=q3, col=col: e.scalar_tensor_tensor(
                    out=qo[q3][:], in0=psQ[k][:], scalar=gqk[:, col:col + 1], in1=sd[k][:], op0=ALU.mult, op1=ALU.mult),
                    reads=[("psQ", k), ("sd", k), ("gqk", col, 0), ("gqk", col, 1)], writes=[("qo", q3)])
                dstT = (self.QT if c < 8 else self.KT)[c % 8]
                rec.op("sp", lambda e, q3=q3, dstT=dstT, ti=ti: e.dma_start(out=dstT[:, ti * 512:(ti + 1) * 512], in_=qo[q3][:]),
                       reads=[("qo", q3)], writes=[("QK_dram", c, ti)], dma=True)
            for b in range(4):
                g = ti * 4 + b
                v2 = g % 2
                for hf in range(2):
                    for kc in range(KC):
                        rec.op("pe", lambda e, hf=hf, kc=kc, b=b, tb=tb: e.matmul(
                            psV[hf][:], lhsT=hT[tb][:, kc, b * 128:(b + 1) * 128],
                            rhs=self.Win[:, kc, 2048 + hf * 512:2048 + (hf + 1) * 512], start=(kc == 0), stop=(kc == KC - 1)),
                            reads=[("hT", tb, b), ("aWin", t, kc)], writes=[("psV", hf)], arena_reads=[0])
                    rec.op("act", lambda e, hf=hf, v2=v2: e.copy(out=vo[v2][:, hf * 512:(hf + 1) * 512], in_=psV[hf][:]),
                           reads=[("psV", hf)], writes=[("vo", v2, hf)])
                rec.op("sp", lambda e, v2=v2, g=g: e.dma_start(out=self.V[g * 128:(g + 1) * 128, :], in_=vo[v2][:]),
                       reads=[("vo", v2, 0), ("vo", v2, 1)], writes=[("V_dram", g)], dma=True)

    def core(self):
        cx, rec, t, p = self.cx, self.cx.rec, self.tag, self.p
        sb = cx.sb
        S = cx.ntok
        nkb = S // 128
        ntile = S // 512
        sb.reset(cx.work0)
        lv = sb.alloc([128, 4, HD], F32, "lv")
        lt = sb.alloc([128, 8], F32, "lt")
        gsub = sb.alloc([128, 1], F32, "gsub")
        A1 = cx.arena[1]
        o1 = NH * D
        KTs = [sb.alloc([128, S], BF16, "KTs")[:], A1[:, o1:o1 + S]]
        Vs = [sb.alloc([128, nkb, 128], BF16, "Vs")[:], A1[:, o1 + S:o1 + 2 * S].rearrange("p (b f) -> p b f", f=128)]
        assert o1 + 2 * S <= ARENA_BYTES // 2
        QTs = [sb.alloc([128, 512], BF16, "QTs") for _ in range(2)]
        pT = [sb.alloc([128, 2, 512], BF16, "pT") for _ in range(3)]
        rl = [sb.alloc([128, 512], F32, "rl") for _ in range(2)]
        oc = [sb.alloc([128, 512], F32, "oc") for _ in range(2)]
        osq = sb.alloc([128, 512], BF16, "osq")
        osd = sb.alloc([128, 512], F32, "osd")
        on = [sb.alloc([128, 512], BF16, "on") for _ in range(2)]
        psS = [[cx.bank[0], cx.bank[1]], [cx.bank[2], cx.bank[3]]]
        psO = [cx.bank[4], cx.bank[5]]
        psL = [cx.bank[6], cx.bank[7]]
        for i, nm in enumerate(("lq1", "lk1", "lq2", "lk2")):
            rec.op("sp", lambda e, i=i, nm=nm: e.dma_start(out=lv[:, i, :], in_=p[nm].partition_broadcast(128)),
                   writes=[("lv", i)], dma=True)
        rec.op("sp", lambda e: e.dma_start(out=gsub[:], in_=p["subln"].rearrange("(d o) -> d o", o=1)), writes=[("gsub",)], dma=True)
        rec.op("dve", lambda e: e.tensor_tensor(out=lv[:, 0, :], in0=lv[:, 0, :], in1=lv[:, 1, :], op=ALU.mult),
               reads=[("lv", 0), ("lv", 1)], writes=[("lv", 0)])
        rec.op("dve", lambda e: e.tensor_tensor(out=lv[:, 2, :], in0=lv[:, 2, :], in1=lv[:, 3, :], op=ALU.mult),
               reads=[("lv", 2), ("lv", 3)], writes=[("lv", 2)])
        rec.op("dve", lambda e: e.reduce_sum(out=lt[:, 0:1], in_=lv[:, 0, :], axis=AX.X), reads=[("lv", 0)], writes=[("lt", 0)])
        rec.op("dve", lambda e: e.reduce_sum(out=lt[:, 1:2], in_=lv[:, 2, :], axis=AX.X), reads=[("lv", 2)], writes=[("lt", 1)])
        rec.op("act", lambda e: e.activation(out=lt[:, 2:4], in_=lt[:, 0:2], func=AF.Exp), reads=[("lt", 0), ("lt", 1)], writes=[("lt", 2)])
        rec.op("dve", lambda e: e.tensor_tensor(out=lt[:, 4:5], in0=lt[:, 3:4], in1=lt[:, 2:3], op=ALU.subtract),
               reads=[("lt", 2)], writes=[("lt", 4)])
        rec.op("dve", lambda e: e.tensor_scalar(out=lt[:, 5:6], in0=lt[:, 4:5], scalar1=-self.lam_init, scalar2=None, op0=ALU.add),
               reads=[("lt", 4)], writes=[("neglam",)])
        rec.op("dve", lambda e: e.tensor_scalar(out=gsub[:], in0=gsub[:], scalar1=1.0 - self.lam_init, scalar2=None, op0=ALU.mult),
               reads=[("gsub",)], writes=[("gsub",)])
        neglam = lt[:, 5:6]

        def load_kv(h):
            hb = h % 2
            ar = [1] if hb == 1 else []
            for q in range(4):
                cs = slice(q * (S // 4), (q + 1) * (S // 4))
                rec.op("sp", lambda e, cs=cs: e.dma_start(out=KTs[hb][:, cs], in_=self.KT[h][:, cs]),
                       reads=[("QK_dram", 8 + h, tt) for tt in range(ntile)], writes=[("KTs", hb, q)], dma=True, arena_writes=ar)
                bs = slice(q * (nkb // 4), (q + 1) * (nkb // 4))
                rec.op("sp", lambda e, bs=bs: e.dma_start(
                    out=Vs[hb][:, bs, :],
                    in_=self.V.rearrange("(b k) f -> k b f", k=128)[:, bs, h * 128:(h + 1) * 128]),
                    reads=[("V_dram", g) for g in range(nkb)], writes=[("Vs", hb, q)], dma=True, arena_writes=ar)

        its = []
        for h in range(NH):
            for ti in range(ntile):
                last = 4 * ti + 3
                for kb in range(last + 1):
                    its.append(dict(h=h, ti=ti, kb=kb, last=last, q0=max(0, kb - 4 * ti) * 128, diag=(kb >= 4 * ti)))

        def rS(i):
            d = its[i]
            h, ti, kb, q0 = d["h"], d["ti"], d["kb"], d["q0"]
            hb, qb, sb2 = h % 2, (h * ntile + ti) % 2, i % 2
            ar = [1] if hb == 1 else []
            if kb == 0:
                rec.op("sp", lambda e: e.dma_start(out=QTs[qb][:], in_=self.QT[h][:, ti * 512:(ti + 1) * 512]),
                       reads=[("QK_dram", h, ti)], writes=[("QTs", qb)], dma=True)
            for c in range(2):
                rec.op("pe", lambda e, c=c: e.matmul(
                    psS[sb2][c][:, q0:512], lhsT=KTs[hb][c * 64:(c + 1) * 64, kb * 128:(kb + 1) * 128],
                    rhs=QTs[qb][c * 64:(c + 1) * 64, q0:512], start=True, stop=True),
                    reads=[("KTs", hb, kb * 128 // (S // 4)), ("QTs", qb)], writes=[("psS", sb2, c)], arena_reads=ar)

        def rE(i):
            d = its[i]
            q0 = d["q0"]
            sb2, p3 = i % 2, i % 3
            for c in range(2):
                rec.op("act", lambda e, c=c: e.activation(
                    out=pT[p3][:, c, q0:512], in_=psS[sb2][c][:, q0:512], func=AF.Exp),
                    reads=[("psS", sb2, c)], writes=[("pT", p3, c)])
            if d["diag"]:
                rec.op("dve", lambda e: e.tensor_tensor(
                    out=pT[p3][:, :, q0:q0 + 128], in0=pT[p3][:, :, q0:q0 + 128],
                    in1=cx.triu[:].unsqueeze(1).broadcast_to([128, 2, 128]), op=ALU.mult),
                    reads=[("pT", p3, 0), ("pT", p3, 1), ("triu",)], writes=[("pT", p3, 0), ("pT", p3, 1)])

        def rPV(i):
            d = its[i]
            h, ti, kb, q0, last = d["h"], d["ti"], d["kb"], d["q0"], d["last"]
            hb, p3 = h % 2, i % 3
            ar = [1] if hb == 1 else []
            if ti == 0 and kb == 0 and h + 1 < NH:
                load_kv(h + 1)
            for c in range(2):
                rec.op("pe", lambda e, c=c: e.matmul(
                    psO[c][:, q0:512], lhsT=Vs[hb][:, kb, :], rhs=pT[p3][:, c, q0:512],
                    start=(kb == 0), stop=(kb == last)),
                    reads=[("Vs", hb, kb // (nkb // 4)), ("pT", p3, c)], writes=[("psO", c)], arena_reads=ar)
                rec.op("pe", lambda e, c=c: e.matmul(
                    psL[c][:, q0:512], lhsT=cx.ones[:], rhs=pT[p3][:, c, q0:512],
                    start=(kb == 0), stop=(kb == last)),
                    reads=[("pT", p3, c), ("ones",)], writes=[("psL", c)])
            if kb == last:
                fin(h, ti)

        def fin(h, ti):
            for c in range(2):
                rec.op("dve", lambda e, c=c: e.reciprocal(out=rl[c][:], in_=psL[c][:]), reads=[("psL", c)], writes=[("rl", c)])
                rec.op("dve", lambda e, c=c: e.tensor_tensor(out=oc[c][:], in0=psO[c][:], in1=rl[c][:], op=ALU.mult),
                       reads=[("psO", c), ("rl", c)], writes=[("oc", c)])
            rec.op("dve", lambda e: e.scalar_tensor_tensor(out=oc[0][:], in0=oc[1][:], scalar=neglam, in1=oc[0][:],
                                                           op0=ALU.mult, op1=ALU.add),
                   reads=[("oc", 0), ("oc", 1), ("neglam",)], writes=[("oc", 0)])
            rec.op("act", lambda e: e.activation(out=osq[:], in_=oc[0][:], func=AF.Square), reads=[("oc", 0)], writes=[("osq",)])
            rec.op("pe", lambda e: e.matmul(psL[0][:], lhsT=cx.ones[:], rhs=osq[:], start=True, stop=True),
                   reads=[("osq",), ("ones",)], writes=[("psL", 0)])
            rec.op("act", lambda e: e.activation(out=osd[:], in_=psL[0][:], func=AF.Sqrt, bias=RMS_EPS, scale=1.0 / 128),
                   reads=[("psL", 0)], writes=[("osd",)])
            rec.op("dve", lambda e: e.reciprocal(out=osd[:], in_=osd[:]), reads=[("osd",)], writes=[("osd",)])
            ob = (h * ntile + ti) % 2
            rec.op("dve", lambda e: e.scalar_tensor_tensor(out=on[ob][:], in0=oc[0][:], scalar=gsub[:, 0:1], in1=osd[:],
                                                           op0=ALU.mult, op1=ALU.mult),
                   reads=[("oc", 0), ("osd",), ("gsub",)], writes=[("on", ob)])
            rec.op("sp", lambda e: e.dma_start(out=self.OT[h][:, ti * 512:(ti + 1) * 512], in_=on[ob][:]),
                   reads=[("on", ob)], writes=[("OT_dram", h, ti)], dma=True)

        load_kv(0)
        rS(0)
        for i in range(len(its)):
            if i + 1 < len(its):
                rS(i + 1)
            rE(i)
            rPV(i)

    def outp(self):
        cx, rec, t, p = self.cx, self.cx.rec, self.tag, self.p
        sb = cx.sb
        S = cx.ntok
        sb.reset(cx.work0)
        xr = [sb.alloc([128, D], F32, "xr") for _ in range(3)]
        oT = [sb.alloc([128, NH, 128], BF16, "oT") for _ in range(3)]
        bank = cx.bank
        psO = [[bank[0], bank[1]], [bank[2], bank[3]]]
        OTv = self.OT.rearrange("h e s -> e h s")
        for b in range(S // 128):
            s3, s2 = b % 3, b % 2
            ti = b // 4
            rec.op("sp", lambda e, s3=s3, b=b: e.dma_start(out=oT[s3][:], in_=OTv[:, :, b * 128:(b + 1) * 128]),
                   reads=[("OT_dram", h, ti) for h in range(NH)], writes=[("oT", s3)], dma=True)
            rec.op("sp", lambda e, s3=s3, b=b: e.dma_start(out=xr[s3][:], in_=self.src[b * 128:(b + 1) * 128, :]),
                   reads=[("X", self.src_n, b)], writes=[("xr", s3)], dma=True)
            for dh in range(2):
                for h in range(NH):
                    rec.op("pe", lambda e, dh=dh, h=h, s3=s3, s2=s2: e.matmul(
                        psO[s2][dh][:], lhsT=oT[s3][:, h, :], rhs=self.Wout[:, h, dh * 512:(dh + 1) * 512],
                        start=(h == 0), stop=(h == NH - 1)),
                        reads=[("oT", s3), ("aWout", t, h // 4)], writes=[("psO", s2, dh)], arena_reads=[1])
                rec.op("dve", lambda e, dh=dh, s3=s3, s2=s2: e.tensor_tensor(
                    out=xr[s3][:, dh * 512:(dh + 1) * 512], in0=psO[s2][dh][:], in1=xr[s3][:, dh * 512:(dh + 1) * 512], op=ALU.add),
                    reads=[("psO", s2, dh), ("xr", s3)], writes=[("xr", s3)])
            rec.op("sp", lambda e, s3=s3, b=b: e.dma_start(out=self.dst[b * 128:(b + 1) * 128, :], in_=xr[s3][:]),
                   reads=[("xr", s3)], writes=[("X", self.dst_n, b)], dma=True)


import math
import ml_dtypes

SEQ = 8192
DEPTH = 4
PARAM_SHAPES = dict(
    ffn1_norm=[4, 1024], ffn1_w_gate_up=[4, 1024, 5632], ffn1_w_down=[4, 2816, 1024], mix_norm=[4, 1024],
    ffn2_norm=[4, 1024], ffn2_w_gate_up=[4, 1024, 5632], ffn2_w_down=[4, 2816, 1024],
    attn_w_in=[2, 1024, 3072], attn_w_out=[2, 1024, 1024], attn_q_norm=[2, 64], attn_k_norm=[2, 64],
    attn_lambda_q1=[2, 64], attn_lambda_k1=[2, 64], attn_lambda_q2=[2, 64], attn_lambda_k2=[2, 64], attn_subln=[2, 128],
    gmlp_w_in=[2, 1024, 4096], gmlp_b_in=[2, 4096], gmlp_ln_g=[2, 2048], gmlp_ln_b=[2, 2048],
    gmlp_w_s=[2, 8, 128, 128], gmlp_b_s=[2, 8, 128], gmlp_w_out=[2, 2048, 1024], gmlp_b_out=[2, 1024])


def build_program(S=SEQ, depth=DEPTH):
    nc = bass.Bass("TRN2", target_bir_lowering=False)
    P = {k: nc.dram_tensor(k, shp, F32, kind="ExternalInput").ap() for k, shp in PARAM_SHAPES.items()}
    x = nc.dram_tensor("x", [S, D], F32, kind="ExternalInput").ap()
    ident = nc.dram_tensor("c_ident", [128, 128], BF16, kind="ExternalInput").ap()
    tril = nc.dram_tensor("c_tril", [128, 128], F32, kind="ExternalInput").ap()
    cb = nc.dram_tensor("c_cb", [3, 128, 128], BF16, kind="ExternalInput").ap()
    out = nc.dram_tensor("out", [S, D], F32, kind="ExternalOutput").ap()
    XA = nc.dram_tensor("XA", [S, D], F32).ap()
    XB = nc.dram_tensor("XB", [S, D], F32).ap()
    scr = (nc.dram_tensor("QT", [8, 128, S], BF16).ap(), nc.dram_tensor("KT", [8, 128, S], BF16).ap(),
           nc.dram_tensor("V", [S, D], BF16).ap(), nc.dram_tensor("OT", [8, 128, S], BF16).ap())
    cx = Ctx(nc, S)
    rec = cx.rec
    load_consts(cx, ident, tril, cb)
    bx, ba, bb, bo = ("xext", x), ("XA", XA), ("XB", XB), ("out", out)
    stages = []
    cur = bx
    for i in range(depth):
        j = i // 2
        last = (i == depth - 1)
        f10 = FFNHalf(cx, 0, P["ffn1_w_gate_up"][i], P["ffn1_w_down"][i], P["ffn1_norm"][i], 0, cur, cur, bb, f"f1a{i}")
        f11 = FFNHalf(cx, 1, P["ffn1_w_gate_up"][i], P["ffn1_w_down"][i], P["ffn1_norm"][i], 1, cur, bb, ba, f"f1b{i}")
        stages.append(("ffn", {0}, f10.load, f10.compute))
        stages.append(("ffn", {1}, f11.load, f11.compute))
        cur = ba
        if i % 2 == 0:
            lam_init = 0.8 - 0.6 * math.exp(-0.3 * i)
            at = Attn(cx, P["attn_w_in"][j], P["attn_w_out"][j], P["attn_q_norm"][j], P["attn_k_norm"][j],
                      P["attn_lambda_q1"][j], P["attn_lambda_k1"][j], P["attn_lambda_q2"][j], P["attn_lambda_k2"][j],
                      P["attn_subln"][j], lam_init, P["mix_norm"][i], ba, ba, scr, f"at{i}")
            stages.append(("aproj", {0, 1}, at.load, at.proj))
            stages.append(("acore", set(), None, at.core))
            stages.append(("aout", {1}, None, at.outp))
        else:
            gm = GMLP(cx, P["gmlp_w_in"][j], P["gmlp_b_in"][j], P["gmlp_ln_g"][j], P["gmlp_ln_b"][j], P["gmlp_w_s"][j],
                      P["gmlp_b_s"][j], P["gmlp_w_out"][j], P["gmlp_b_out"][j], P["mix_norm"][i], ba, ba, f"gm{i}")
            stages.append(("gmlp", {0, 1}, gm.load, gm.compute))
        f20 = FFNHalf(cx, 0, P["ffn2_w_gate_up"][i], P["ffn2_w_down"][i], P["ffn2_norm"][i], 0, ba, ba, bb, f"f2a{i}")
        f21 = FFNHalf(cx, 1, P["ffn2_w_gate_up"][i], P["ffn2_w_down"][i], P["ffn2_norm"][i], 1, ba, bb, bo if last else ba, f"f2b{i}")
        stages.append(("ffn", {0}, f20.load, f20.compute))
        stages.append(("ffn", {1}, f21.load, f21.compute))
    loaded = [False] * len(stages)

    def busy_after(k):
        b = set()
        for m in range(k, len(stages)):
            if loaded[m] and stages[m][2] is not None:
                b |= stages[m][1] if stages[m][0] != "aproj" else {0, 1}
        return b

    prev_kind = None
    for k, (kind, arenas, load, compute) in enumerate(stages):
        if load is not None and not loaded[k]:
            load()
            loaded[k] = True
        if load is None:
            loaded[k] = True
        if prev_kind is not None and not (prev_kind == "ffn" and kind == "ffn"):
            rec.fence()
        nk = k + 1
        while nk < len(stages) and stages[nk][2] is None:
            nk += 1
        if nk < len(stages) and not loaded[nk]:
            need = stages[nk][1]
            inuse = set(arenas)
            if kind == "aproj":
                inuse = {0, 1}
            elif kind == "acore":
                inuse = {1}
            elif kind == "aout":
                inuse = {1}
            if not (need & inuse):
                stages[nk][2]()
                loaded[nk] = True
        compute()
        prev_kind = kind
    rec.emit()
    return nc, cx


_CACHE = {}


def kernel(**inputs):
    x = np.ascontiguousarray(np.asarray(inputs["x"], dtype=np.float32))
    B = x.shape[0]
    if "nc" not in _CACHE:
        _CACHE["nc"] = build_program()[0]
    nc = _CACHE["nc"]
    bf = ml_dtypes.bfloat16
    blk = np.zeros((128, 128), np.float32)
    blk[:64, :64] = 1
    blk[64:, 64:] = 1
    consts = dict(c_ident=np.eye(128, dtype=np.float32).astype(bf), c_tril=np.tril(np.ones((128, 128), np.float32)),
                  c_cb=np.stack([np.triu(np.ones((128, 128), np.float32)), np.ones((128, 128), np.float32), blk]).astype(bf))
    params = {k: np.ascontiguousarray(np.asarray(inputs[k], dtype=np.float32)) for k in PARAM_SHAPES}
    in_maps = [dict(x=x[b], **params, **consts) for b in range(B)]
    res = run_bass_kernel_spmd(nc, in_maps, core_ids=list(range(B)))
    return np.stack([np.asarray(r["out"], dtype=np.float32) for r in res.results], axis=0)
```
